# Optimizing a Trainium2 kernel written in Bass

```python
import math
import jax, jax.numpy as jnp
from jax import lax
import numpy as np

D_MODEL = 1024
BATCH = 8
SEQ = 4096
DEPTH = 1

N_HEADS = 8
HEAD_DIM = 64
V_DIM = 2 * HEAD_DIM
QK_WIDTH = N_HEADS * 2 * HEAD_DIM
ATTN_WIDTH = N_HEADS * V_DIM
LRU_WIDTH = D_MODEL
LRU_BLOCKS = 8
LRU_BLOCK = LRU_WIDTH // LRU_BLOCKS
CONV_WIDTH = 4
LRU_C = 8.0
D_FF = 4 * D_MODEL
N_BUCKETS = 32
MAX_DISTANCE = 128
Q_BLOCK = 128
EPS = 1e-6
IN_WIDTH = 3 * QK_WIDTH // 2 * 0 + QK_WIDTH + QK_WIDTH + ATTN_WIDTH + 2 * LRU_WIDTH + 2 * D_MODEL

kernel_name = 'hybrid_diffattn_rglru_gated_block'


def rmsnorm(x, g):
    x32 = x.astype(jnp.float32)
    y = x32 * lax.rsqrt(jnp.mean(jnp.square(x32), axis=-1, keepdims=True) + EPS)
    return (y * g.astype(jnp.float32)).astype(x.dtype)


def rel_bucket(n):
    max_exact = N_BUCKETS // 2
    n = jnp.maximum(n, 0)
    nf = jnp.maximum(n, 1).astype(jnp.float32)
    large = max_exact + (jnp.log(nf / max_exact) / math.log(MAX_DISTANCE / max_exact)
                         * (N_BUCKETS - max_exact)).astype(jnp.int32)
    large = jnp.minimum(large, N_BUCKETS - 1)
    return jnp.where(n < max_exact, n, large)


def diff_attention(q, k, v, q_g, k_g, rel_bias, lam, subln_g, lambda_init):
    B, S = q.shape[0], q.shape[1]
    q = rmsnorm(q, q_g)
    k = rmsnorm(k, k_g)
    nb = S // Q_BLOCK
    qb = jnp.transpose(q.reshape(B, nb, Q_BLOCK, N_HEADS, 2, HEAD_DIM), (1, 0, 2, 3, 4, 5))
    k_pos = jnp.arange(S, dtype=jnp.int32)
    scale = HEAD_DIM ** -0.5
    bias_tab = rel_bias.astype(jnp.float32)

    def one_block(args):
        q_blk, blk = args
        q_pos = blk * Q_BLOCK + jnp.arange(Q_BLOCK, dtype=jnp.int32)
        rel = q_pos[:, None] - k_pos[None, :]
        bias = jnp.transpose(bias_tab[rel_bucket(rel)], (2, 0, 1))
        logits = jnp.einsum('bqhcd,bkhcd->bhcqk', q_blk, k).astype(jnp.float32) * scale
        logits = logits + bias[None, :, None]
        logits = jnp.where(rel >= 0, logits, -jnp.inf)
        p = jax.nn.softmax(logits, axis=-1)
        attn = p[:, :, 0] - lam * p[:, :, 1]
        return jnp.einsum('bhqk,bkhe->bqhe', attn.astype(v.dtype), v)

    out = lax.map(one_block, (qb, jnp.arange(nb, dtype=jnp.int32)))
    out = jnp.transpose(out, (1, 0, 2, 3, 4)).reshape(B, S, N_HEADS, V_DIM)
    out = rmsnorm(out, subln_g) * (1.0 - lambda_init)
    return out.reshape(B, S, ATTN_WIDTH)


def rg_lru_branch(xr, gr, conv_w, conv_b, w_a, b_a, w_x, b_x, lru_lambda):
    B, S, W = xr.shape
    gate = jax.nn.gelu(gr)
    xpad = jnp.pad(xr, ((0, 0), (CONV_WIDTH - 1, 0), (0, 0)))
    xc = conv_b
    for tap in range(CONV_WIDTH):
        xc = xc + xpad[:, tap:tap + S] * conv_w[tap]
    xblk = xc.reshape(B, S, LRU_BLOCKS, LRU_BLOCK)
    r = jax.nn.sigmoid(jnp.einsum('bsne,nef->bsnf', xblk, w_a) + b_a).reshape(B, S, W)
    i = jax.nn.sigmoid(jnp.einsum('bsne,nef->bsnf', xblk, w_x) + b_x).reshape(B, S, W)
    log_a = -LRU_C * r.astype(jnp.float32) * jax.nn.softplus(-lru_lambda.astype(jnp.float32))
    a = jnp.exp(log_a)
    mult = jnp.sqrt(-jnp.expm1(2.0 * log_a))
    b = mult * (i * xc).astype(jnp.float32)

    def combine(c1, c2):
        a1, b1 = c1
        a2, b2 = c2
        return a1 * a2, a2 * b1 + b2

    _, h = lax.associative_scan(combine, (a, b), axis=1)
    return (h.astype(xr.dtype)) * gate


def setup_inputs(seed: int = 0) -> dict:
    key = jax.random.key(seed)
    ks = jax.random.split(key, 24)
    f32 = jnp.float32
    nrm = lambda k, shape, scale: jax.random.normal(k, shape, f32) * scale
    gain = lambda k, shape: 1.0 + 0.05 * jax.random.normal(k, shape, f32)
    x = jax.random.normal(ks[0], (BATCH, SEQ, D_MODEL), f32)
    u = jax.random.uniform(ks[15], (DEPTH, LRU_WIDTH), f32, minval=0.9, maxval=0.999)
    a0 = u ** (1.0 / LRU_C)
    lru_lambda = jnp.log(a0) - jnp.log1p(-a0)
    return {
        'x': x,
        'norm_mix_g': gain(ks[1], (DEPTH, D_MODEL)),
        'w_in': nrm(ks[2], (DEPTH, D_MODEL, IN_WIDTH), D_MODEL ** -0.5),
        'q_norm_g': gain(ks[3], (DEPTH, HEAD_DIM)),
        'k_norm_g': gain(ks[4], (DEPTH, HEAD_DIM)),
        'lambda_q1': nrm(ks[5], (DEPTH, HEAD_DIM), 0.1),
        'lambda_k1': nrm(ks[6], (DEPTH, HEAD_DIM), 0.1),
        'lambda_q2': nrm(ks[7], (DEPTH, HEAD_DIM), 0.1),
        'lambda_k2': nrm(ks[8], (DEPTH, HEAD_DIM), 0.1),
        'subln_g': gain(ks[9], (DEPTH, V_DIM)),
        'conv_w': nrm(ks[10], (DEPTH, CONV_WIDTH, LRU_WIDTH), CONV_WIDTH ** -0.5),
        'conv_b': nrm(ks[11], (DEPTH, LRU_WIDTH), 0.02),
        'w_rg_a': nrm(ks[12], (DEPTH, LRU_BLOCKS, LRU_BLOCK, LRU_BLOCK), LRU_BLOCK ** -0.5),
        'b_rg_a': nrm(ks[13], (DEPTH, LRU_BLOCKS, LRU_BLOCK), 0.02),
        'w_rg_x': nrm(ks[14], (DEPTH, LRU_BLOCKS, LRU_BLOCK, LRU_BLOCK), LRU_BLOCK ** -0.5),
        'b_rg_x': nrm(ks[16], (DEPTH, LRU_BLOCKS, LRU_BLOCK), 0.02),
        'lru_lambda': lru_lambda,
        'w_proj_attn': nrm(ks[17], (DEPTH, ATTN_WIDTH, D_MODEL), ATTN_WIDTH ** -0.5),
        'w_proj_rnn': nrm(ks[18], (DEPTH, LRU_WIDTH, D_MODEL), LRU_WIDTH ** -0.5),
        'w_out': nrm(ks[19], (DEPTH, D_MODEL, D_MODEL), D_MODEL ** -0.5),
        'norm_mlp_g': gain(ks[20], (DEPTH, D_MODEL)),
        'w_up': nrm(ks[21], (DEPTH, D_MODEL, D_FF), D_MODEL ** -0.5),
        'w_down': nrm(ks[22], (DEPTH, D_FF, D_MODEL), D_FF ** -0.5),
        'rel_bias': nrm(ks[23], (N_BUCKETS, N_HEADS), 0.5),
    }


def reference(x, norm_mix_g, w_in, q_norm_g, k_norm_g, lambda_q1, lambda_k1, lambda_q2, lambda_k2,
              subln_g, conv_w, conv_b, w_rg_a, b_rg_a, w_rg_x, b_rg_x, lru_lambda,
              w_proj_attn, w_proj_rnn, w_out, norm_mlp_g, w_up, w_down, rel_bias):
    B, S, _ = x.shape
    offs = np.cumsum([QK_WIDTH, QK_WIDTH, ATTN_WIDTH, LRU_WIDTH, LRU_WIDTH, D_MODEL]).tolist()
    for l in range(DEPTH):
        lambda_init = 0.8 - 0.6 * math.exp(-0.3 * l)
        h = rmsnorm(x, norm_mix_g[l])
        proj = jnp.einsum('bsd,de->bse', h, w_in[l])
        q, k, v, xr, gr, g_attn, g_rnn = jnp.split(proj, offs, axis=-1)
        q = q.reshape(B, S, N_HEADS, 2, HEAD_DIM)
        k = k.reshape(B, S, N_HEADS, 2, HEAD_DIM)
        v = v.reshape(B, S, N_HEADS, V_DIM)
        lam = (jnp.exp(jnp.sum(lambda_q1[l].astype(jnp.float32) * lambda_k1[l].astype(jnp.float32)))
               - jnp.exp(jnp.sum(lambda_q2[l].astype(jnp.float32) * lambda_k2[l].astype(jnp.float32)))
               + lambda_init)
        attn_out = diff_attention(q, k, v, q_norm_g[l], k_norm_g[l], rel_bias, lam, subln_g[l], lambda_init)
        rnn_out = rg_lru_branch(xr, gr, conv_w[l], conv_b[l], w_rg_a[l], b_rg_a[l],
                                w_rg_x[l], b_rg_x[l], lru_lambda[l])
        y_attn = jnp.einsum('bse,ed->bsd', attn_out, w_proj_attn[l])
        y_rnn = jnp.einsum('bse,ed->bsd', rnn_out, w_proj_rnn[l])
        merged = jax.nn.sigmoid(g_attn) * y_attn + jax.nn.sigmoid(g_rnn) * y_rnn
        x = x + jnp.einsum('bsd,de->bse', merged, w_out[l])
        h2 = rmsnorm(x, norm_mlp_g[l])
        up = jnp.square(jax.nn.relu(jnp.einsum('bsd,df->bsf', h2, w_up[l])))
        x = x + jnp.einsum('bsf,fd->bsd', up, w_down[l])
    return x
```

```python
import math
from contextlib import ExitStack
import numpy as np
import concourse.bass as bass
import concourse.mybir as mybir
from concourse.bass_utils import run_bass_kernel_spmd

F32 = mybir.dt.float32
BF16 = mybir.dt.bfloat16
AF = mybir.ActivationFunctionType
ALU = mybir.AluOpType

S = 4096
D = 1024
NT = S // 128
NCH = S // 512
H = 8
DFF = 4096
EPS = 1e-6
NEG = -30000.0
NDUMMY = 2
C_GQ, C_GK, C_SUB, C_CW, C_CB, C_BA, C_BX, C_LAM, NCOLS = 0, 1, 2, 3, 35, 43, 51, 59, 67
D_GK, D_GSUB, D_NLAM, D_HBA, D_HBX, D_M4, D_M8, NDC = 0, 1, 2, 3, 11, 19, 27, 35


class Buf:
    def __init__(self, name):
        self.name = name
        self.w = {}
        self.r = {}
        self.dsem = None
        self.dval = 0


class K:
    def __init__(self, nc, es):
        self.nc = nc
        self.es = es
        self.engs = {"pe": nc.tensor, "act": nc.scalar, "dve": nc.vector, "pool": nc.gpsimd, "sp": nc.sync}
        self.sems = {n: es.enter_context(nc.semaphore("s_" + n)) for n in self.engs}
        self.cnt = {n: 0 for n in self.engs}
        self.seen = {n: {} for n in self.engs}
        self.nbuf = 0
        self.stores = []

    def buf(self, name="b"):
        self.nbuf += 1
        return Buf(f"{name}{self.nbuf}")

    def bufs(self, n, name="b"):
        return [self.buf(name) for _ in range(n)]

    def _deps(self, reads, writes, partial):
        deps = {}

        def add(d):
            for k, v in d.items():
                if k not in deps:
                    deps[k] = v
                elif isinstance(k, str):
                    deps[k] = max(deps[k], v)
                elif v[1] > deps[k][1]:
                    deps[k] = v

        for b in reads:
            add(b.w)
        for b in writes:
            add(b.r)
            if not partial:
                add(b.w)
        return deps

    def _wait(self, eng, deps):
        e = self.engs[eng]
        seen = self.seen[eng]
        for k, v in deps.items():
            if isinstance(k, str):
                if seen.get(k, 0) >= v:
                    continue
                e.wait_ge(self.sems[k], v)
                seen[k] = v
            else:
                if seen.get(k, 0) >= v[1]:
                    continue
                e.wait_ge(v[0], v[1])
                seen[k] = v[1]

    def _mark(self, key, val, reads, writes, partial):
        for b in reads:
            b.r[key] = val
        for b in writes:
            if not partial:
                b.w = {}
            b.w[key] = val
            b.r = {}

    def op(self, eng, fns, reads=(), writes=(), partial=False):
        self._wait(eng, self._deps(reads, writes, partial))
        e = self.engs[eng]
        if not isinstance(fns, (list, tuple)):
            fns = [fns]
        ins = None
        for f in fns:
            ins = f(e)
        self.cnt[eng] += 1
        ins.then_inc(self.sems[eng], 1)
        self._mark(eng, self.cnt[eng], reads, writes, partial)

    def dma(self, eng, out, in_, owner, reads=(), writes=(), partial=False):
        self._wait(eng, self._deps(reads, writes, partial))
        if owner.dsem is None:
            owner.dsem = self.es.enter_context(self.nc.semaphore("d_" + owner.name))
        ins = self.engs[eng].dma_start(out=out, in_=in_)
        owner.dval += 16
        ins.then_inc(owner.dsem, 16)
        self._mark(("d", owner.name), (owner.dsem, owner.dval), reads, writes, partial)

    def fence(self, bufs):
        for b in bufs:
            for n in ("pe", "act", "dve", "pool"):
                if self.cnt[n]:
                    b.r[n] = self.cnt[n]


def bcast_rows(ap, nrows, off, n):
    return bass.AP(ap.tensor, off, [[0, nrows], [1, n]])


def build_nc():
    nc = bass.Bass("TRN2", target_bir_lowering=False)
    dt_in = lambda name, shape: nc.dram_tensor(name, shape, F32, kind="ExternalInput").ap()
    x = dt_in("x", [S, D])
    w_in = dt_in("w_in", [D, 7168])
    w_pa = dt_in("w_pa", [D, D])
    w_pr = dt_in("w_pr", [D, D])
    w_out = dt_in("w_out", [D, D])
    w_up = dt_in("w_up", [D, DFF])
    w_down = dt_in("w_down", [DFF, D])
    w_rg = dt_in("w_rg", [128, 2 * 8 * 128])
    cols_d = dt_in("cols", [128, NCOLS])
    cst_d = dt_in("cst", [128, 512])
    oht_d = dt_in("oht", [33, 383])
    relb = dt_in("rel_bias", [32, 8])
    lamv_d = dt_in("lamv", [1, 256])
    gmix_d = dt_in("gmix", [1, D])
    gmlp_d = dt_in("gmlp", [1, D])
    subg_d = dt_in("subg", [1, 128])
    y = nc.dram_tensor("y", [S, D], F32, kind="ExternalOutput").ap()
    internal = lambda name, shape, dt: nc.dram_tensor(name, shape, dt, kind="Internal").ap()
    vscr = internal("vscr", [8, 383], F32)
    winr_t = internal("winr_t", [32, 128, 8, 128], BF16)
    wpa_t = internal("wpa_t", [8, 128, 8, 128], BF16)
    wpr_t = internal("wpr_t", [8, 128, 8, 128], BF16)
    wup_t = internal("wup_t", [32, 128, 8, 128], BF16)
    wout_t = internal("wout_t", [128, 8, 1024], BF16)
    wdown_t = internal("wdown_t", [128, 32, 1024], BF16)

    with ExitStack() as es:
        k = K(nc, es)
        uid = [0]

        def sb(st, name, shape, dt):
            uid[0] += 1
            return st.enter_context(nc.sbuf_tensor(f"s{uid[0]}_{name}", shape, dt))

        def ps(st, name, shape, dt):
            uid[0] += 1
            return st.enter_context(nc.psum_tensor(f"p{uid[0]}_{name}", shape, dt))

        AT = sb(es, "AT", [128, 8, S], BF16)
        ATb = [k.bufs(NCH, "AT") for _ in range(H)]
        cst = sb(es, "cst", [128, 4, 128], BF16)
        cstb = k.buf("cst")
        ident, bd64, m128, ones = cst[:, 0, :], cst[:, 1, :], cst[:, 2, :], cst[:, 3, :]
        cols = sb(es, "cols", [128, NCOLS], F32)
        colsb = k.buf("cols")
        dcol = sb(es, "dcol", [128, NDC], F32)
        dcolb = k.buf("dcol")
        bias31 = sb(es, "bias31", [128, 8], F32)
        bias31b = k.buf("bias31")
        gmix = sb(es, "gmix", [128, D], F32)
        gmixb = k.buf("gmix")
        gmlp = sb(es, "gmlp", [128, D], F32)
        gmlpb = k.buf("gmlp")
        wrg = sb(es, "wrg", [128, 2, 8, 128], BF16)
        wrgb = k.buf("wrg")
        gsubT = sb(es, "gsubT", [128, 128], F32)
        gsubTb = k.buf("gsubT")

        k.dma("sp", cols[:], cols_d[:], colsb, writes=[colsb])
        k.dma("pool", cst[:].rearrange("p a b -> p (a b)"), cst_d[:], cstb, writes=[cstb])
        k.dma("sp", bias31[:], bcast_rows(relb, 128, 31 * 8, 8), bias31b, writes=[bias31b])
        k.dma("sp", gmix[:], bcast_rows(gmix_d, 128, 0, D), gmixb, writes=[gmixb])
        k.dma("sp", gmlp[:], bcast_rows(gmlp_d, 128, 0, D), gmlpb, writes=[gmlpb])
        k.dma("pool", wrg[:].rearrange("p a c f -> p (a c f)"), w_rg[:], wrgb, writes=[wrgb])
        k.dma("sp", gsubT[:], bcast_rows(subg_d, 128, 0, 128), gsubTb, writes=[gsubTb])
        k.op("dve", lambda e: e.tensor_scalar(out=gsubT[:], in0=gsubT[:], scalar1=0.8, scalar2=None, op0=ALU.mult), reads=[gsubTb], writes=[gsubTb])

        vsb = k.buf("vscr")
        with ExitStack() as s0:
            lamv = sb(s0, "lamv", [128, 256], F32)
            lamvb = k.buf("lamv")
            rb33 = sb(s0, "rb33", [33, 8], F32)
            rb33b = k.buf("rb33")
            oht = sb(s0, "oht", [33, 383], F32)
            ohtb = k.buf("oht")
            vecs = sb(s0, "vecs", [8, 383], F32)
            vecsb = k.buf("vecs")
            tmpc = sb(s0, "tmpc", [128, 64], F32)
            tmpcb = k.buf("tmpc")
            junk = sb(s0, "junk", [128, 64], F32)
            junkb = k.buf("junk")
            vps = ps(s0, "vps", [8, 383], F32)
            vpsb = k.buf("vps")
            k.dma("sp", lamv[:], bcast_rows(lamv_d, 128, 0, 256), lamvb, writes=[lamvb])
            k.op("dve", lambda e: e.memset(rb33[:], NEG), writes=[rb33b])
            k.dma("sp", rb33[0:32, :], relb[:], rb33b, writes=[rb33b])
            k.dma("sp", oht[:], oht_d[:], ohtb, writes=[ohtb])
            k.op("dve", lambda e: e.tensor_tensor(out=dcol[:, D_GK:D_GK + 1], in0=cols[:, C_GQ:C_GQ + 1],
                                                  in1=cols[:, C_GK:C_GK + 1], op=ALU.mult), reads=[colsb], writes=[dcolb])
            k.op("dve", lambda e: e.tensor_scalar(out=dcol[:, D_GSUB:D_GSUB + 1], in0=cols[:, C_SUB:C_SUB + 1],
                                                  scalar1=0.8, scalar2=None, op0=ALU.mult), reads=[colsb], writes=[dcolb], partial=True)
            k.op("dve", lambda e: e.tensor_tensor(out=junk[:, 0:64], in0=lamv[:, 0:64], in1=lamv[:, 64:128], op=ALU.mult),
                 reads=[lamvb], writes=[junkb])
            k.op("dve", lambda e: e.reduce_sum(out=tmpc[:, 0:1], in_=junk[:, 0:64], axis=mybir.AxisListType.X),
                 reads=[junkb], writes=[tmpcb])
            k.op("dve", lambda e: e.tensor_tensor(out=junk[:, 0:64], in0=lamv[:, 128:192], in1=lamv[:, 192:256], op=ALU.mult),
                 reads=[lamvb], writes=[junkb])
            k.op("dve", lambda e: e.reduce_sum(out=tmpc[:, 1:2], in_=junk[:, 0:64], axis=mybir.AxisListType.X),
                 reads=[junkb], writes=[tmpcb], partial=True)
            k.op("act", lambda e: e.activation(out=tmpc[:, 2:4], in_=tmpc[:, 0:2], func=AF.Exp), reads=[tmpcb], writes=[tmpcb])
            k.op("dve", lambda e: e.tensor_tensor(out=tmpc[:, 4:5], in0=tmpc[:, 3:4], in1=tmpc[:, 2:3], op=ALU.subtract),
                 reads=[tmpcb], writes=[tmpcb])
            k.op("dve", lambda e: e.tensor_scalar(out=dcol[:, D_NLAM:D_NLAM + 1], in0=tmpc[:, 4:5], scalar1=-0.2, scalar2=None,
                                                  op0=ALU.add), reads=[tmpcb], writes=[dcolb], partial=True)
            k.op("dve", lambda e: e.tensor_scalar(out=dcol[:, D_HBA:D_HBA + 16], in0=cols[:, C_BA:C_BA + 16], scalar1=0.5,
                                                  scalar2=None, op0=ALU.mult), reads=[colsb], writes=[dcolb], partial=True)
            yv, zv, z2, acc = tmpc[:, 8:16], tmpc[:, 16:24], tmpc[:, 24:32], tmpc[:, 32:40]
            k.op("act", lambda e: e.activation(out=yv, in_=cols[:, C_LAM:C_LAM + 8], func=AF.Exp, scale=-1.0),
                 reads=[colsb], writes=[tmpcb])
            k.op("dve", lambda e: e.tensor_scalar(out=zv, in0=yv, scalar1=2.0, scalar2=None, op0=ALU.add), reads=[tmpcb], writes=[tmpcb])
            k.op("dve", lambda e: e.reciprocal(out=zv, in_=zv), reads=[tmpcb], writes=[tmpcb])
            k.op("dve", lambda e: e.tensor_tensor(out=zv, in0=zv, in1=yv, op=ALU.mult), reads=[tmpcb], writes=[tmpcb])
            k.op("dve", lambda e: e.tensor_tensor(out=z2, in0=zv, in1=zv, op=ALU.mult), reads=[tmpcb], writes=[tmpcb])
            k.op("dve", lambda e: e.memset(acc, 1.0 / 19.0), reads=[tmpcb], writes=[tmpcb])
            for n in (17, 15, 13, 11, 9, 7, 5, 3, 1):
                k.op("dve", lambda e: e.tensor_tensor(out=acc, in0=acc, in1=z2, op=ALU.mult), reads=[tmpcb], writes=[tmpcb])
                k.op("dve", lambda e, n=n: e.tensor_scalar(out=acc, in0=acc, scalar1=1.0 / n, scalar2=None, op0=ALU.add),
                     reads=[tmpcb], writes=[tmpcb])
            k.op("dve", lambda e: e.tensor_tensor(out=acc, in0=acc, in1=zv, op=ALU.mult), reads=[tmpcb], writes=[tmpcb])
            k.op("dve", lambda e: e.tensor_scalar(out=dcol[:, D_M4:D_M4 + 8], in0=acc, scalar1=-8.0, scalar2=None, op0=ALU.mult),
                 reads=[tmpcb], writes=[dcolb], partial=True)
            k.op("dve", lambda e: e.tensor_scalar(out=dcol[:, D_M8:D_M8 + 8], in0=acc, scalar1=-16.0, scalar2=None, op0=ALU.mult),
                 reads=[tmpcb], writes=[dcolb], partial=True)
            k.op("pe", lambda e: e.matmul(vps[:], lhsT=rb33[:], rhs=oht[:], start=True, stop=True), reads=[rb33b, ohtb], writes=[vpsb])
            k.op("dve", lambda e: e.tensor_copy(out=vecs[:], in_=vps[:]), reads=[vpsb], writes=[vecsb])
            k.dma("sp", vscr[:], vecs[:], vsb, reads=[vecsb], writes=[vsb])
            k.op("dve", lambda e: e.memset(junk[:, 0:1], 0.0), reads=[vsb, tmpcb, lamvb, rb33b, ohtb], writes=[junkb])
        setup_fence = dict(k.cnt)

        winrb, wpab, wprb, wupb, woutb, wdownb = (k.buf(n) for n in ("winr", "wpa", "wpr", "wup", "wout", "wdown"))
        conv_jobs = []
        for cc in range(32):
            conv_jobs.append((winr_t[cc], w_in[:, 3072 + 128 * cc:3072 + 128 * (cc + 1)].rearrange("(c p) n -> p c n", p=128), winrb))
        for j in range(8):
            conv_jobs.append((wpa_t[j], w_pa[:, 128 * j:128 * (j + 1)].rearrange("(c p) n -> p c n", p=128), wpab))
            conv_jobs.append((wpr_t[j], w_pr[:, 128 * j:128 * (j + 1)].rearrange("(c p) n -> p c n", p=128), wprb))
        conv_jobs.append((wout_t[:], w_out.rearrange("(c p) n -> p c n", p=128), woutb))
        for f in range(32):
            conv_jobs.append((wup_t[f], w_up[:, 128 * f:128 * (f + 1)].rearrange("(c p) n -> p c n", p=128), wupb))
        for q4 in range(4):
            conv_jobs.append((wdown_t[:, 8 * q4:8 * (q4 + 1), :],
                              w_down[1024 * q4:1024 * (q4 + 1), :].rearrange("(c p) n -> p c n", p=128), wdownb))

        def issue_conv(n):
            for _ in range(n):
                if conv_jobs:
                    o, i, b = conv_jobs.pop(0)
                    k.dma("pool", o, i, b, writes=[b], partial=True)

        def norm_transpose(st_x, xbuf, gt, gtb, dst_fn, dstb, tp_ps, tpb, xn, xnb, st, stb, junkt, junktb, evac_eng):
            k.op("act", lambda e: e.activation(out=junkt[:], in_=st_x, func=AF.Square, accum_out=st[:, 0:1]),
                 reads=[xbuf], writes=[junktb, stb])
            k.op("act", lambda e: e.activation(out=st[:, 1:2], in_=st[:, 0:1], func=AF.Ln, bias=EPS, scale=1.0 / D),
                 reads=[stb], writes=[stb])
            k.op("act", lambda e: e.activation(out=st[:, 2:3], in_=st[:, 1:2], func=AF.Exp, scale=-0.5), reads=[stb], writes=[stb])
            k.op("dve", lambda e: e.scalar_tensor_tensor(out=xn[:], in0=st_x, scalar=st[:, 2:3], in1=gt[:], op0=ALU.mult, op1=ALU.mult),
                 reads=[xbuf, stb, gtb], writes=[xnb])
            k.op("pe", [lambda e, c=c: e.transpose(out=tp_ps[:, c, :], in_=xn[:, 128 * c:128 * (c + 1)], identity=ident) for c in range(8)],
                 reads=[xnb, cstb], writes=[tpb])
            k.op(evac_eng, lambda e: e.tensor_copy(out=dst_fn(), in_=tp_ps[:]) if evac_eng != "act" else e.copy(out=dst_fn(), in_=tp_ps[:]),
                 reads=[tpb], writes=[dstb], partial=True)

        with ExitStack() as s12:
            hT = sb(s12, "hT", [128, 8, S], BF16)
            hTb = k.bufs(NCH, "hT")
            with ExitStack() as s1:
                xt = [sb(s1, f"xt{i}", [128, D], F32) for i in range(3)]
                xtb = k.bufs(3, "xt")
                xn = [sb(s1, f"xn{i}", [128, D], BF16) for i in range(2)]
                xnb = k.bufs(2, "xn")
                st = [sb(s1, f"st{i}", [128, 4], F32) for i in range(2)]
                stb = k.bufs(2, "st")
                junkt = sb(s1, "junkt", [128, D], BF16)
                junktb = k.buf("junkt")
                tp = [ps(s1, f"tp{i}", [128, 8, 128], BF16) for i in range(2)]
                tpb = k.bufs(2, "tp")
                k.fence(xtb + xnb + stb + [junktb] + tpb)
                issue_conv(0)
                for i in range(NT):
                    s3i, s2i = i % 3, i % 2
                    k.dma("sp", xt[s3i][:], x[128 * i:128 * (i + 1), :], xtb[s3i], writes=[xtb[s3i]])
                    norm_transpose(xt[s3i][:], xtb[s3i], gmix, gmixb,
                                   lambda i=i: hT[:, :, 128 * i:128 * (i + 1)], hTb[i // 4],
                                   tp[s2i], tpb[s2i], xn[s2i], xnb[s2i], st[s2i], stb[s2i], junkt, junktb,
                                   "dve" if i % 2 == 0 else "act")
                k.op("dve", lambda e: e.memset(st[0][:, 3:4], 0.0), reads=xtb + xnb + [junktb] + tpb + stb, writes=[stb[0]])

            with ExitStack() as s2:
                qT = sb(s2, "qT", [128, S], BF16)
                qTb = k.bufs(NCH, "qT")
                kT = sb(s2, "kT", [128, S], BF16)
                kTb = k.bufs(NCH, "kT")
                vv = sb(s2, "vv", [128, NT, 129], BF16)
                vvb = k.bufs(NCH, "vv")
                wqkv = [sb(s2, f"wqkv{i}", [128, 3, 8, 128], BF16) for i in range(2)]
                wqkvb = k.bufs(2, "wqkv")
                Pt = [sb(s2, f"P{i}", [128, 2, 512], BF16) for i in range(3)]
                Ptb = k.bufs(3, "P")
                tmpdb = []
                sq = [sb(s2, f"sq{i}", [128, 512], BF16) for i in range(2)]
                sqb = k.bufs(2, "sq")
                rs = [sb(s2, f"rs{i}", [128, 512], F32) for i in range(2)]
                rsb = k.bufs(2, "rs")
                zz = sb(s2, "zz", [128, 16], F32)
                zzb = k.buf("zz")
                tmpo = sb(s2, "tmpo", [128, 128], F32)
                tmpob = k.buf("tmpo")
                dd = sb(s2, "dd", [128, 4, 128], F32)
                ddb = k.buf("dd")
                abf = sb(s2, "abf", [128, 4, 128], BF16)
                abfb = k.buf("abf")
                junk2 = sb(s2, "junk2", [128, 4, 128], F32)
                junk2b = k.buf("junk2")
                hank = [sb(s2, f"hank{i}", [128, 2, 128], F32) for i in range(2)]
                hankb = k.bufs(2, "hank")
                T01 = [sb(s2, f"T01{i}", [128, 2, 128], F32) for i in range(2)]
                T01b = k.bufs(2, "T01")
                E01 = [sb(s2, f"E01{i}", [128, 2, 128], F32) for i in range(2)]
                E01b = k.bufs(2, "E01")
                nb31 = sb(s2, "nb31", [128, 8], F32)
                nb31b = k.buf("nb31")
                ozc = sb(s2, "ozc", [128, 3, 387], F32)
                ozcb = k.buf("ozc")
                PS = [ps(s2, f"PS{i}", [128, 2, 512], F32) for i in range(2)]
                OZ = ps(s2, "OZ", [128, 3, 512], F32)
                TPp = ps(s2, "TPp", [128, 4, 128], BF16)
                PS.append(OZ)
                PSb = k.bufs(3, "PS")
                OZb = PSb[2]
                TPb = k.buf("TPp")
                k.fence(qTb + kTb + vvb + wqkvb + Ptb + tmpdb + sqb + rsb + [zzb, ddb, tmpob, abfb, junk2b, TPb] + hankb + T01b + E01b + [nb31b, ozcb] + PSb)
                k.op("dve", lambda e: e.memset(vv[:, :, 128:129], 1.0), writes=vvb)
                k.op("dve", lambda e: e.tensor_scalar(out=nb31[:], in0=bias31[:], scalar1=-1.0, scalar2=None, op0=ALU.mult), reads=[bias31b], writes=[nb31b])

                def acc(i, c):
                    a_ = 2 * i + c
                    return OZ[:, a_ // 3, 129 * (a_ % 3):129 * (a_ % 3) + 129]

                def load_head_w(h):
                    sl = h % 2
                    for wi in range(3):
                        c0 = 1024 * wi + 128 * h
                        k.dma("pool", wqkv[sl][:, wi, :, :], w_in[:, c0:c0 + 128].rearrange("(c p) n -> p c n", p=128),
                              wqkvb[sl], writes=[wqkvb[sl]], partial=(wi > 0))
                    k.dma("sp", hank[sl][:, 0, :], bass.AP(vscr.tensor, 383 * h + 128, [[1, 128], [1, 128]]), hankb[sl],
                          reads=[vsb], writes=[hankb[sl]])
                    k.dma("sp", hank[sl][:, 1, :], bass.AP(vscr.tensor, 383 * h, [[1, 128], [1, 128]]), hankb[sl],
                          reads=[vsb], writes=[hankb[sl]], partial=True)

                load_head_w(0)
                gcnt = [0, 0, 0, 0]
                scnt = [0]
                cur_slot = [0]
                for h in range(H):
                    sl = h % 2
                    if h + 1 < H:
                        load_head_w(h + 1)
                    issue_conv(11)
                    k.op("dve", lambda e: e.tensor_copy(out=T01[sl][:], in_=bass.AP(hank[sl], 127, [[256, 128], [128, 2], [-1, 128]])),
                         reads=[hankb[sl]], writes=[T01b[sl]])
                    k.op("act", lambda e: e.activation(out=E01[sl][:], in_=T01[sl][:], func=AF.Exp, bias=nb31[:, h:h + 1], scale=1.0),
                         reads=[T01b[sl], nb31b], writes=[E01b[sl]])
                    def proj_a(wi, t):
                        pi = gcnt[0] % 3
                        gcnt[0] += 1
                        si = gcnt[1] % 2
                        gcnt[1] += 1
                        P_, Pb_ = PS[pi], PSb[pi]
                        k.op("pe", [lambda e, c=c: e.matmul(P_[:, 0, :], lhsT=wqkv[sl][:, wi, c, :], rhs=hT[:, c, 512 * t:512 * (t + 1)],
                                                            start=(c == 0), stop=(c == 7)) for c in range(8)],
                             reads=[wqkvb[sl], hTb[t]], writes=[Pb_])
                        k.op("act", lambda e: e.activation(out=sq[si][:], in_=P_[:, 0, :], func=AF.Square), reads=[Pb_], writes=[sqb[si]])
                        return (wi, t, pi, si)

                    def proj_b(st_):
                        wi, t, pi, si = st_
                        P_, Pb_ = PS[pi], PSb[pi]
                        k.op("pe", lambda e: e.matmul(P_[:, 1, :], lhsT=bd64, rhs=sq[si][:], start=True, stop=True),
                             reads=[sqb[si], cstb], writes=[Pb_], partial=True)
                        k.op("act", lambda e: e.activation(out=rs[si][:], in_=P_[:, 1, :], func=AF.Ln, bias=EPS), reads=[Pb_], writes=[rsb[si]])
                        k.op("act", lambda e: e.activation(out=rs[si][:], in_=rs[si][:], func=AF.Exp, scale=-0.5), reads=[rsb[si]], writes=[rsb[si]])
                        if wi == 0:
                            k.op("dve", lambda e: e.tensor_tensor(out=qT[:, 512 * t:512 * (t + 1)], in0=P_[:, 0, :], in1=rs[si][:], op=ALU.mult),
                                 reads=[Pb_, rsb[si]], writes=[qTb[t]])
                        else:
                            k.op("dve", lambda e: e.scalar_tensor_tensor(out=kT[:, 512 * t:512 * (t + 1)], in0=P_[:, 0, :],
                                                                         scalar=dcol[:, D_GK:D_GK + 1], in1=rs[si][:], op0=ALU.mult, op1=ALU.mult),
                                 reads=[Pb_, rsb[si], dcolb], writes=[kTb[t]])

                    plist = [(wi, t) for wi in range(2) for t in range(NCH)]
                    prev = None
                    for (wi, t) in plist:
                        cur = proj_a(wi, t)
                        if prev is not None:
                            proj_b(prev)
                        prev = cur
                    proj_b(prev)
                    for tg in range(NCH):
                        pi = gcnt[0] % 3
                        gcnt[0] += 1
                        P_, Pb_ = PS[pi], PSb[pi]
                        fl = []
                        for i4 in range(4):
                            tt = 4 * tg + i4
                            for c in range(8):
                                fl.append(lambda e, c=c, i4=i4, tt=tt: e.matmul(P_[:, 0, 128 * i4:128 * (i4 + 1)], lhsT=hT[:, c, 128 * tt:128 * (tt + 1)],
                                                                               rhs=wqkv[sl][:, 2, c, :], start=(c == 0), stop=(c == 7)))
                        k.op("pe", fl, reads=[wqkvb[sl], hTb[tg]], writes=[Pb_])
                        k.op("dve", lambda e: e.tensor_copy(out=vv[:, 4 * tg:4 * tg + 4, 0:128], in_=P_[:, 0, :].rearrange("p (a b) -> p a b", b=128)),
                             reads=[Pb_], writes=[vvb[tg]], partial=True)
                    iters = []
                    for t in range(NCH):
                        full = list(range(0, max(0, 4 * t - 1)))
                        diag = list(range(max(0, 4 * t - 1), 4 * t + 4))
                        order = []
                        if full:
                            order.append(full.pop(0))
                        while full or diag:
                            if diag:
                                order.append(diag.pop(0))
                            if full:
                                order.append(full.pop(0))
                        iters += [(t, j) for j in order]
                    lastj = {}
                    for (t, j) in iters:
                        for i in range(max(0, j - 4 * t), 4):
                            lastj[(t, i)] = j
                    sidx = {}

                    pending = []

                    def emit_qk(n):
                        t, j = iters[n]
                        si = scnt[0] % 2
                        scnt[0] += 1
                        sidx[n] = si
                        c0 = max(0, 128 * (j - 4 * t))
                        k.op("pe", [lambda e, c=c: e.matmul(PS[si][:, c, c0:512], lhsT=kT[64 * c:64 * (c + 1), 128 * j:128 * (j + 1)],
                                                            rhs=qT[64 * c:64 * (c + 1), 512 * t + c0:512 * (t + 1)], start=True, stop=True)
                                    for c in range(2)],
                             reads=[kTb[j // 4], qTb[t]], writes=[PSb[si]])

                    emit_qk(0)
                    emit_qk(1)
                    for n, (t, j) in enumerate(iters):
                        si = sidx[n]
                        cur_slot[0] = si
                        Sp, Spb = PS[si], PSb[si]
                        pi = gcnt[2] % 3
                        gcnt[2] += 1
                        Pp, Ppb = Pt[pi], Ptb[pi]
                        jj = j - 4 * t
                        bcol = bias31[:, h:h + 1]
                        c0 = max(0, 128 * jj)
                        k.op("act", lambda e: e.activation(out=Pp[:, :, c0:512], in_=Sp[:, :, c0:512], func=AF.Exp, bias=bcol, scale=0.125),
                             reads=[Spb, bias31b], writes=[Ppb])
                        if jj >= -1:
                            if jj == -1:
                                cb, toff, nb = 0, 128, 1
                            elif jj == 3:
                                cb, toff, nb = 384, 0, 1
                            else:
                                cb, toff, nb = 128 * jj, 0, 2
                            k.op("dve", lambda e: e.tensor_tensor(out=Pp[:, :, cb:cb + 128 * nb], in0=Pp[:, :, cb:cb + 128 * nb],
                                                                  in1=bass.AP(E01[sl], toff, [[256, 128], [0, 2], [1, 128 * nb]]), op=ALU.mult),
                                 reads=[Ppb, E01b[sl]], writes=[Ppb])
                        if n + 2 < len(iters):
                            emit_qk(n + 2)
                        first = (n == 0 or iters[n - 1][0] != t)
                        fl = []
                        for i in range(max(0, jj), 4):
                            for c in range(2):
                                fl.append(lambda e, i=i, c=c: e.matmul(acc(i, c), lhsT=Pp[:, c, 128 * i:128 * (i + 1)], rhs=vv[:, j, :],
                                                                       start=(first and (2 * i + c) % 3 == 0), stop=(j == lastj[(t, i)]),
                                                                       skip_group_check=True))
                        for _ in range(NDUMMY):
                            fl.append(lambda e: e.matmul(OZ[:, 2, 258:512], lhsT=ident, rhs=kT[:, 0:254], start=False, stop=False, skip_group_check=True))
                        k.op("pe", fl, reads=[Ppb, vvb[j // 4], cstb], writes=[OZb], partial=not first)
                        last = (n == len(iters) - 1 or iters[n + 1][0] != t)
                        if last:
                            k.op("dve", lambda e: e.tensor_copy(out=ozc[:], in_=OZ[:, :, 0:387]), reads=[OZb], writes=[ozcb])
                            zcols = bass.AP(ozc, 128, [[1161, 128], [387, 3], [129, 3]])

                            def accs(i, c):
                                a_ = 2 * i + c
                                return ozc[:, a_ // 3, 129 * (a_ % 3):129 * (a_ % 3) + 128]

                            k.op("dve", lambda e: e.reciprocal(out=zz[:, 0:9].rearrange("p (a b) -> p a b", b=3), in_=zcols), reads=[ozcb], writes=[zzb])
                            k.op("dve", lambda e: e.tensor_scalar(out=zz[:, 0:8].rearrange("p (i c) -> p i c", c=2)[:, :, 1:2],
                                                                  in0=zz[:, 0:8].rearrange("p (i c) -> p i c", c=2)[:, :, 1:2],
                                                                  scalar1=dcol[:, D_NLAM:D_NLAM + 1], scalar2=None, op0=ALU.mult),
                                 reads=[zzb, dcolb], writes=[zzb])
                            for i in range(4):
                                k.op("dve", lambda e, i=i: e.tensor_scalar(out=tmpo[:], in0=accs(i, 0), scalar1=zz[:, 2 * i:2 * i + 1], scalar2=None,
                                                                           op0=ALU.mult), reads=[ozcb, zzb], writes=[tmpob])
                                k.op("dve", lambda e, i=i: e.scalar_tensor_tensor(out=dd[:, i, :], in0=accs(i, 1), scalar=zz[:, 2 * i + 1:2 * i + 2],
                                                                                  in1=tmpo[:], op0=ALU.mult, op1=ALU.add),
                                     reads=[ozcb, zzb, tmpob], writes=[ddb], partial=(i > 0))

                            def part2a(t=t):
                                k.op("dve", lambda e: e.tensor_tensor(out=junk2[:], in0=dd[:], in1=dd[:], op=ALU.mult), reads=[ddb], writes=[junk2b])
                                k.op("dve", lambda e: e.reduce_sum(out=zz[:, 8:12], in_=junk2[:], axis=mybir.AxisListType.X), reads=[junk2b], writes=[zzb], partial=True)
                                k.op("act", lambda e: e.activation(out=zz[:, 12:16], in_=zz[:, 8:12], func=AF.Ln, bias=EPS, scale=1.0 / 128), reads=[zzb], writes=[zzb])
                                k.op("act", lambda e: e.activation(out=zz[:, 12:16], in_=zz[:, 12:16], func=AF.Exp, scale=-0.5), reads=[zzb], writes=[zzb])
                                for i in range(4):
                                    k.op("dve", lambda e, i=i: e.scalar_tensor_tensor(out=abf[:, i, :], in0=dd[:, i, :], scalar=zz[:, 12 + i:13 + i], in1=gsubT[:],
                                                                                      op0=ALU.mult, op1=ALU.mult),
                                         reads=[ddb, zzb, gsubTb], writes=[abfb], partial=(i > 0))

                            def part2b(t=t):
                                k.op("pe", [lambda e, i=i: e.transpose(out=TPp[:, i, :], in_=abf[:, i, :], identity=ident) for i in range(4)],
                                     reads=[abfb, cstb], writes=[TPb])
                                k.op("dve", lambda e: e.tensor_copy(out=AT[:, h, 512 * t:512 * (t + 1)], in_=TPp[:].rearrange("p a b -> p (a b)")),
                                     reads=[TPb], writes=[ATb[h][t]])

                            pending.append((n + 3, part2a))
                            pending.append((n + 7, part2b))
                        while pending and (pending[0][0] <= n or n == len(iters) - 1):
                            pending.pop(0)[1]()
                issue_conv(1000)
                k.op("dve", lambda e: e.memset(dd[:, 0:1], 0.0),
                     reads=qTb + kTb + vvb + wqkvb + Ptb + tmpdb + sqb + rsb + [zzb, tmpob, abfb, junk2b, TPb, ozcb, nb31b] + hankb + T01b + E01b + PSb + hTb, writes=[ddb])

        with ExitStack() as s3:
            xres = [sb(s3, f"xres{i}", [128, D], F32) for i in range(4)]
            xresb = k.bufs(4, "xres")
            mgT = sb(s3, "mgT", [128, 8, 512], BF16)
            mgTb = k.bufs(8, "mgT")
            RT = sb(s3, "RT", [128, 8, 512], BF16)
            RTb = k.bufs(8, "RT")
            xn = sb(s3, "xn3", [128, D], BF16)
            xnb = k.buf("xn3")
            junkt = sb(s3, "junk3", [128, D], BF16)
            junktb = k.buf("junk3")
            st = sb(s3, "st3", [128, 4], F32)
            stb = k.buf("st3")
            carry = sb(s3, "carry", [128, 8, 4], F32)
            carryb = k.bufs(8, "carry")
            wA = [sb(s3, f"wA{i}", [128, 8, 128], BF16) for i in range(6)]
            wAb = k.bufs(6, "wA")
            wO = sb(s3, "wO", [128, 8, 512], BF16)
            wOb = k.buf("wO")
            wD = [sb(s3, f"wD{i}", [128, 4, 512], BF16) for i in range(2)]
            wDb = k.bufs(2, "wD")
            tp = ps(s3, "tp3", [128, 8, 128], BF16)
            tpb = k.buf("tp3")
            G = [ps(s3, f"G{i}", [128, 512], F32) for i in range(7)]
            Gb = k.bufs(7, "G")
            k.fence(xresb + mgTb + RTb + [xnb, junktb, stb] + carryb + wAb + [wOb] + wDb + [tpb] + Gb)
            k.op("dve", lambda e: e.memset(carry[:], 0.0), writes=carryb)
            wa_i = [0]

            def load_wA(src, srcb):
                i = wa_i[0] % 6
                wa_i[0] += 1
                k.dma("sp", wA[i][:], src, wAb[i], reads=[srcb], writes=[wAb[i]])
                return wA[i], wAb[i]

            hTc = sb(s3, "hTc", [128, 8, 512], BF16)
            hTcb = k.bufs(1, "hTc")
            xs = [sb(s3, f"xs{i}", [128, D], F32) for i in range(1)]
            xsb = k.bufs(1, "xs")
            k.fence(hTcb + xsb)
            for t in range(NCH):
                tok0 = 512 * t
                with ExitStack() as sa:
                    tnames = ["xr", "xc", "xcb16", "xg", "tr", "ti", "tg", "aa", "a2", "hs"]
                    TT = [{}, {}]
                    BB = [{}, {}]
                    for nm in tnames:
                        single = nm in ("xr", "tr", "hs")
                        for par in range(2):
                            if single and par == 1:
                                TT[1][nm], BB[1][nm] = TT[0][nm], BB[0][nm]
                                continue
                            TT[par][nm] = sb(sa, nm, [128, 515 if nm == "xr" else 512], BF16 if nm == "xcb16" else F32)
                            BB[par][nm] = k.buf(nm)
                    ta = sb(sa, "ta", [128, 512], F32)
                    m2 = sb(sa, "m2", [128, 512], F32)
                    tnP = sb(sa, "tnP", [128, 8, 512], BF16)
                    m1P = sb(sa, "m1P", [128, 8, 512], BF16)
                    B = {n: k.buf(n) for n in ("ta", "m2")}
                    tnPb = k.bufs(8, "tnP")
                    m1Pb = k.bufs(8, "m1P")
                    allb = list(B.values()) + list(BB[0].values()) + list(BB[1].values()) + tnPb + m1Pb
                    k.fence(allb)
                    for i in range(4):
                        k.dma("sp", xres[i][:], x[tok0 + 128 * i:tok0 + 128 * (i + 1), :], xresb[i], writes=[xresb[i]])
                    if t == 0:
                        for i in range(4):
                            norm_transpose(xres[i][:], xresb[i], gmix, gmixb, lambda i=i: hTc[:, :, 128 * i:128 * (i + 1)], hTcb[0],
                                           tp, tpb, xn, xnb, st, stb, junkt, junktb, "dve")

                    def rnn_s1a(c):
                        T_, B_ = TT[c % 2], BB[c % 2]
                        xr, xc, xcb16, xg, tg_ = T_["xr"], T_["xc"], T_["xcb16"], T_["xg"], T_["tg"]
                        wx, wxb = load_wA(winr_t[c], winrb)
                        wg, wgb = load_wA(winr_t[8 + c], winrb)
                        k.op("pe", [lambda e, d=d: e.matmul(G[0][:], lhsT=wx[:, d, :], rhs=hTc[:, d, :], start=(d == 0), stop=(d == 7)) for d in range(8)],
                             reads=[wxb, hTcb[0]], writes=[Gb[0]])
                        k.op("pe", [lambda e, d=d: e.matmul(G[1][:], lhsT=wg[:, d, :], rhs=hTc[:, d, :], start=(d == 0), stop=(d == 7)) for d in range(8)],
                             reads=[wgb, hTcb[0]], writes=[Gb[1]])
                        k.op("act", lambda e: e.copy(out=xr[:, 3:515], in_=G[0][:]), reads=[Gb[0]], writes=[B_["xr"]])
                        k.op("act", lambda e: e.copy(out=xg[:], in_=G[1][:]), reads=[Gb[1]], writes=[B_["xg"]])
                        k.op("pool", lambda e: e.tensor_tensor(out=tg_[:], in0=xg[:], in1=xg[:], op=ALU.mult), reads=[B_["xg"]], writes=[B_["tg"]])
                        k.op("pool", lambda e: e.tensor_scalar(out=tg_[:], in0=tg_[:], scalar1=0.044715, scalar2=1.0, op0=ALU.mult, op1=ALU.add),
                             reads=[B_["tg"]], writes=[B_["tg"]])
                        k.op("pool", lambda e: e.tensor_tensor(out=tg_[:], in0=tg_[:], in1=xg[:], op=ALU.mult), reads=[B_["tg"], B_["xg"]], writes=[B_["tg"]])
                        k.op("dve", lambda e: e.tensor_copy(out=xr[:, 0:3], in_=carry[:, c, 0:3]), reads=[carryb[c]], writes=[B_["xr"]], partial=True)
                        cw = lambda tap: cols[:, C_CW + 8 * tap + c:C_CW + 8 * tap + c + 1]
                        k.op("dve", lambda e: e.tensor_scalar(out=xc[:], in0=xr[:, 0:512], scalar1=cw(0), scalar2=cols[:, C_CB + c:C_CB + c + 1],
                                                              op0=ALU.mult, op1=ALU.add), reads=[B_["xr"], colsb], writes=[B_["xc"]])
                        for tap in (1, 2, 3):
                            k.op("dve", lambda e, tap=tap: e.scalar_tensor_tensor(out=xc[:], in0=xr[:, tap:tap + 512], scalar=cw(tap), in1=xc[:],
                                                                                  op0=ALU.mult, op1=ALU.add), reads=[B_["xr"], B_["xc"], colsb], writes=[B_["xc"]])
                        k.op("dve", lambda e: e.tensor_copy(out=carry[:, c, 0:3], in_=xr[:, 512:515]), reads=[B_["xr"]], writes=[carryb[c]])
                        k.op("pool", lambda e: e.tensor_copy(out=xcb16[:], in_=xc[:]), reads=[B_["xc"]], writes=[B_["xcb16"]])

                    def rnn_s2(c):
                        T_, B_ = TT[c % 2], BB[c % 2]
                        xcb16, xg, tr, ti_, tg_, aa, a2 = T_["xcb16"], T_["xg"], T_["tr"], T_["ti"], T_["tg"], T_["aa"], T_["a2"]
                        k.op("pe", lambda e: e.matmul(G[2][:], lhsT=wrg[:, 0, c, :], rhs=xcb16[:], start=True, stop=True),
                             reads=[wrgb, B_["xcb16"]], writes=[Gb[2]])
                        k.op("pe", lambda e: e.matmul(G[3][:], lhsT=wrg[:, 1, c, :], rhs=xcb16[:], start=True, stop=True),
                             reads=[wrgb, B_["xcb16"]], writes=[Gb[3]])
                        k.op("act", lambda e: e.activation(out=tr[:], in_=G[2][:], func=AF.Tanh, bias=dcol[:, D_HBA + c:D_HBA + c + 1], scale=0.5),
                             reads=[Gb[2], dcolb], writes=[B_["tr"]])
                        k.op("act", lambda e: e.activation(out=ti_[:], in_=G[3][:], func=AF.Tanh, bias=dcol[:, D_HBX + c:D_HBX + c + 1], scale=0.5),
                             reads=[Gb[3], dcolb], writes=[B_["ti"]])
                        k.op("act", lambda e: e.activation(out=tg_[:], in_=tg_[:], func=AF.Tanh, scale=math.sqrt(2.0 / math.pi)),
                             reads=[B_["tg"]], writes=[B_["tg"]])
                        k.op("act", lambda e: e.activation(out=aa[:], in_=tr[:], func=AF.Exp, bias=dcol[:, D_M4 + c:D_M4 + c + 1],
                                                           scale=dcol[:, D_M4 + c:D_M4 + c + 1]), reads=[B_["tr"], dcolb], writes=[B_["aa"]])
                        k.op("act", lambda e: e.activation(out=a2[:], in_=tr[:], func=AF.Exp, bias=dcol[:, D_M8 + c:D_M8 + c + 1],
                                                           scale=dcol[:, D_M8 + c:D_M8 + c + 1]), reads=[B_["tr"], dcolb], writes=[B_["a2"]])
                        k.op("act", lambda e: e.activation(out=a2[:], in_=a2[:], func=AF.Sqrt, bias=1.0, scale=-1.0), reads=[B_["a2"]], writes=[B_["a2"]])

                    def rnn_s3(c):
                        T_, B_ = TT[c % 2], BB[c % 2]
                        xc, xg, ti_, tg_, aa, a2, hs = T_["xc"], T_["xg"], T_["ti"], T_["tg"], T_["aa"], T_["a2"], T_["hs"]
                        k.op("dve", lambda e: e.scalar_tensor_tensor(out=ti_[:], in0=ti_[:], scalar=1.0, in1=xc[:], op0=ALU.add, op1=ALU.mult),
                             reads=[B_["ti"], B_["xc"]], writes=[B_["ti"]])
                        k.op("dve", lambda e: e.tensor_tensor(out=ti_[:], in0=ti_[:], in1=a2[:], op=ALU.mult), reads=[B_["ti"], B_["a2"]], writes=[B_["ti"]])
                        k.op("dve", lambda e: e.tensor_tensor_scan(out=hs[:], data0=aa[:], data1=ti_[:], initial=carry[:, c, 3:4], op0=ALU.mult, op1=ALU.add),
                             reads=[B_["aa"], B_["ti"], carryb[c]], writes=[B_["hs"]])
                        k.op("dve", lambda e: e.tensor_copy(out=carry[:, c, 3:4], in_=hs[:, 511:512]), reads=[B_["hs"]], writes=[carryb[c]])
                        k.op("dve", lambda e: e.scalar_tensor_tensor(out=tg_[:], in0=tg_[:], scalar=1.0, in1=xg[:], op0=ALU.add, op1=ALU.mult),
                             reads=[B_["tg"], B_["xg"]], writes=[B_["tg"]])
                        k.op("dve", lambda e: e.scalar_tensor_tensor(out=RT[:, c, :], in0=hs[:], scalar=0.25, in1=tg_[:], op0=ALU.mult, op1=ALU.mult),
                             reads=[B_["hs"], B_["tg"]], writes=[RTb[c]])

                    def merge_early(j):
                        wga_s, wga_b = load_wA(winr_t[16 + j], winrb)
                        wgn_s, wgn_b = load_wA(winr_t[24 + j], winrb)
                        wpa_s, wpa_b = load_wA(wpa_t[j], wpab)
                        k.op("pe", [lambda e, d=d: e.matmul(G[4][:], lhsT=wga_s[:, d, :], rhs=hTc[:, d, :], start=(d == 0), stop=(d == 7)) for d in range(8)],
                             reads=[wga_b, hTcb[0]], writes=[Gb[4]])
                        k.op("pe", [lambda e, d=d: e.matmul(G[5][:], lhsT=wgn_s[:, d, :], rhs=hTc[:, d, :], start=(d == 0), stop=(d == 7)) for d in range(8)],
                             reads=[wgn_b, hTcb[0]], writes=[Gb[5]])
                        k.op("pe", [lambda e, d=d: e.matmul(G[6][:], lhsT=wpa_s[:, d, :], rhs=AT[:, d, tok0:tok0 + 512], start=(d == 0), stop=(d == 7)) for d in range(8)],
                             reads=[wpa_b] + [ATb[d][t] for d in range(8)], writes=[Gb[6]])
                        k.op("act", lambda e: e.activation(out=ta[:], in_=G[4][:], func=AF.Tanh, scale=0.5), reads=[Gb[4]], writes=[B["ta"]])
                        k.op("act", lambda e: e.activation(out=tnP[:, j, :], in_=G[5][:], func=AF.Tanh, scale=0.5), reads=[Gb[5]], writes=[tnPb[j]])
                        k.op("dve", lambda e: e.scalar_tensor_tensor(out=m1P[:, j, :], in0=ta[:], scalar=1.0, in1=G[6][:], op0=ALU.add, op1=ALU.mult),
                             reads=[B["ta"], Gb[6]], writes=[m1Pb[j]])

                    for s_ in range(10):
                        if 0 <= s_ - 2 < 8:
                            rnn_s3(s_ - 2)
                        if s_ < 8:
                            rnn_s1a(s_)
                        if 0 <= s_ - 1 < 8:
                            rnn_s2(s_ - 1)
                        if 0 <= s_ - 2 < 8:
                            merge_early(s_ - 2)
                    for j in range(8):
                        wpr_s, wpr_b = load_wA(wpr_t[j], wprb)
                        g = j % 2
                        k.op("pe", [lambda e, d=d: e.matmul(G[g][:], lhsT=wpr_s[:, d, :], rhs=RT[:, d, :], start=(d == 0), stop=(d == 7)) for d in range(8)],
                             reads=[wpr_b] + RTb, writes=[Gb[g]])
                        k.op("dve", lambda e: e.scalar_tensor_tensor(out=m2[:], in0=tnP[:, j, :], scalar=1.0, in1=G[g][:], op0=ALU.add, op1=ALU.mult),
                             reads=[tnPb[j], Gb[g]], writes=[B["m2"]])
                        k.op("dve", lambda e: e.tensor_tensor(out=mgT[:, j, :], in0=m2[:], in1=m1P[:, j, :], op=ALU.add), reads=[B["m2"], m1Pb[j]], writes=[mgTb[j]])
                    for half in range(2):
                        k.dma("sp", wO[:], wout_t[:, :, 512 * half:512 * (half + 1)], wOb, reads=[woutb], writes=[wOb])
                        for i in range(4):
                            g = 4 + (2 * half + i) % 2
                            k.op("pe", [lambda e, d=d: e.matmul(G[g][:], lhsT=mgT[:, d, 128 * i:128 * (i + 1)], rhs=wO[:, d, :], start=(d == 0), stop=(d == 7)) for d in range(8)],
                                 reads=[wOb] + mgTb, writes=[Gb[g]])
                            k.op("dve", lambda e: e.scalar_tensor_tensor(out=xres[i][:, 512 * half:512 * (half + 1)], in0=G[g][:], scalar=0.5,
                                                                         in1=xres[i][:, 512 * half:512 * (half + 1)], op0=ALU.mult, op1=ALU.add),
                                 reads=[Gb[g], xresb[i]], writes=[xresb[i]])
                    for i in range(4):
                        norm_transpose(xres[i][:], xresb[i], gmlp, gmlpb, lambda i=i: RT[:, :, 128 * i:128 * (i + 1)], RTb[i],
                                       tp, tpb, xn, xnb, st, stb, junkt, junktb, "dve")
                    k.op("dve", lambda e: e.memset(st[:, 3:4], 0.0), reads=RTb[0:4] + allb, writes=RTb[4:8] + [stb])
                with ExitStack() as sm:
                    uT = sb(sm, "uT", [128, 32, 512], BF16)
                    uTb = k.bufs(32, "uT")
                    rl = [sb(sm, f"rl{i}", [128, 512], F32) for i in range(2)]
                    rlb = k.bufs(2, "rl")
                    k.fence(uTb + rlb)
                    for f in range(32):
                        wu, wub = load_wA(wup_t[f], wupb)
                        g = 4 + f % 3
                        k.op("pe", [lambda e, d=d: e.matmul(G[g][:], lhsT=wu[:, d, :], rhs=RT[:, d, :], start=(d == 0), stop=(d == 7)) for d in range(8)],
                             reads=[wub] + RTb, writes=[Gb[g]])
                        k.op("act", lambda e: e.activation(out=rl[f % 2][:], in_=G[g][:], func=AF.Relu), reads=[Gb[g]], writes=[rlb[f % 2]])
                        k.op("pool", lambda e: e.tensor_tensor(out=uT[:, f, :], in0=rl[f % 2][:], in1=rl[f % 2][:], op=ALU.mult),
                             reads=[rlb[f % 2]], writes=[uTb[f]])
                    if t + 1 < NCH:
                        for i in range(4):
                            k.dma("sp", xs[0][:], x[tok0 + 512 + 128 * i:tok0 + 512 + 128 * (i + 1), :], xsb[0], writes=[xsb[0]])
                            norm_transpose(xs[0][:], xsb[0], gmix, gmixb, lambda i=i: hTc[:, :, 128 * i:128 * (i + 1)], hTcb[0],
                                           tp, tpb, xn, xnb, st, stb, junkt, junktb, "dve")
                    for half in range(2):
                        for f4 in range(8):
                            wi = (half * 8 + f4) % 2
                            k.dma("sp", wD[wi][:], wdown_t[:, 4 * f4:4 * f4 + 4, 512 * half:512 * (half + 1)], wDb[wi], reads=[wdownb], writes=[wDb[wi]])
                            for fi in range(4):
                                f = 4 * f4 + fi
                                k.op("pe", [lambda e, i=i: e.matmul(G[i][:], lhsT=uT[:, f, 128 * i:128 * (i + 1)], rhs=wD[wi][:, fi, :],
                                                                    start=(f == 0), stop=(f == 31)) for i in range(4)],
                                     reads=[wDb[wi], uTb[f]], writes=Gb[0:4], partial=(f != 0))
                        for i in range(4):
                            k.op("dve", lambda e: e.tensor_tensor(out=xres[i][:, 512 * half:512 * (half + 1)], in0=G[i][:],
                                                                  in1=xres[i][:, 512 * half:512 * (half + 1)], op=ALU.add),
                                 reads=[Gb[i], xresb[i]], writes=[xresb[i]])
                    for i in range(4):
                        k.dma("sp", y[tok0 + 128 * i:tok0 + 128 * (i + 1), :], xres[i][:], xresb[i], reads=[xresb[i]])
                    k.op("dve", lambda e: e.memset(st[:, 3:4], 0.0), reads=uTb + rlb, writes=[stb])
            k._wait("sp", {kk: vv_ for b in xresb for kk, vv_ in b.r.items()})
            k.op("dve", lambda e: e.memset(st[:, 3:4], 0.0), reads=xresb + mgTb + RTb + wAb + [wOb] + wDb + [tpb] + Gb + carryb + hTcb + xsb, writes=[stb])
    import os
    if os.environ.get('KDEBUG'):
        print('counts', k.cnt, 'ninstr', nc.n_instructions if not callable(nc.n_instructions) else nc.n_instructions())
    return nc


_NC_CACHE = {}


def _host_consts():
    cst = np.zeros((128, 4, 128), np.float32)
    cst[:, 0, :] = np.eye(128, dtype=np.float32)
    bd = np.zeros((128, 128), np.float32)
    bd[:64, :64] = 1.0 / 64
    bd[64:, 64:] = 1.0 / 64
    cst[:, 1, :] = bd
    cst[:, 2, :] = 1.0 / 128
    cst[:, 3, :] = 1.0
    oht = np.zeros((33, 383), np.float32)
    for m in range(383):
        n = 255 - m
        if n < 0:
            b = 32
        elif n < 16:
            b = n
        else:
            nf = np.float32(n)
            b = 16 + int(np.float32(np.log(nf / np.float32(16)) / np.float32(math.log(128 / 16)) * np.float32(16)))
            b = min(b, 31)
        oht[b, m] = 1.0
    return cst.reshape(128, 512), oht


def kernel(**inputs):
    f = lambda name: np.ascontiguousarray(np.asarray(inputs[name], dtype=np.float32))
    x = f("x")
    col = lambda v: np.ascontiguousarray(v.reshape(-1, 128).T)
    cols = np.zeros((128, NCOLS), np.float32)
    cols[:, C_GQ] = np.tile(f("q_norm_g")[0], 2)
    cols[:, C_GK] = np.tile(f("k_norm_g")[0], 2)
    cols[:, C_SUB] = f("subln_g")[0]
    cw = f("conv_w")[0]
    for tap in range(4):
        cols[:, C_CW + 8 * tap:C_CW + 8 * tap + 8] = col(cw[tap])
    cols[:, C_CB:C_CB + 8] = col(f("conv_b")[0])
    cols[:, C_BA:C_BA + 8] = col(f("b_rg_a")[0].reshape(-1))
    cols[:, C_BX:C_BX + 8] = col(f("b_rg_x")[0].reshape(-1))
    cols[:, C_LAM:C_LAM + 8] = col(f("lru_lambda")[0])
    wrg = np.stack([f("w_rg_a")[0], f("w_rg_x")[0]], axis=0)
    wrg = np.ascontiguousarray(wrg.transpose(2, 0, 1, 3).reshape(128, 2 * 8 * 128))
    lamv = np.concatenate([f("lambda_q1")[0], f("lambda_k1")[0], f("lambda_q2")[0], f("lambda_k2")[0]]).reshape(1, 256)
    cst, oht = _host_consts()
    shared = {
        "w_in": f("w_in")[0], "w_pa": f("w_proj_attn")[0], "w_pr": f("w_proj_rnn")[0], "w_out": f("w_out")[0],
        "w_up": f("w_up")[0], "w_down": f("w_down")[0], "w_rg": wrg, "cols": cols, "cst": cst, "oht": oht,
        "rel_bias": f("rel_bias"), "lamv": np.ascontiguousarray(lamv), "gmix": f("norm_mix_g"), "gmlp": f("norm_mlp_g"), "subg": f("subln_g"),
    }
    if "nc" not in _NC_CACHE:
        _NC_CACHE["nc"] = build_nc()
    nc = _NC_CACHE["nc"]
    in_maps = [dict(shared, x=np.ascontiguousarray(x[b])) for b in range(8)]
    res = run_bass_kernel_spmd(nc, in_maps, core_ids=list(range(8)))
    return np.stack([np.asarray(r["y"], dtype=np.float32) for r in res.results], axis=0)
```

```python
import math
from contextlib import ExitStack
import numpy as np
import concourse.bass as bass
import concourse.mybir as mybir
from concourse.bass_utils import run_bass_kernel_spmd

F32 = mybir.dt.float32
BF16 = mybir.dt.bfloat16
AF = mybir.ActivationFunctionType
ALU = mybir.AluOpType

S = 4096
D = 1024
NT = S // 128
NCH = S // 512
H = 8
DFF = 4096
EPS = 1e-6
NEG = -30000.0
NDUMMY = 2
C_GQ, C_GK, C_SUB, C_CW, C_CB, C_BA, C_BX, C_LAM, NCOLS = 0, 1, 2, 3, 35, 43, 51, 59, 67
D_GK, D_GSUB, D_NLAM, D_HBA, D_HBX, D_M4, D_M8, NDC = 0, 1, 2, 3, 11, 19, 27, 35


class Buf:
    def __init__(self, name):
        self.name = name
        self.w = {}
        self.r = {}
        self.dsem = None
        self.dval = 0


class K:
    def __init__(self, nc, es):
        self.nc = nc
        self.es = es
        self.engs = {"pe": nc.tensor, "act": nc.scalar, "dve": nc.vector, "pool": nc.gpsimd, "sp": nc.sync}
        self.sems = {n: es.enter_context(nc.semaphore("s_" + n)) for n in self.engs}
        self.cnt = {n: 0 for n in self.engs}
        self.seen = {n: {} for n in self.engs}
        self.nbuf = 0
        self.stores = []

    def buf(self, name="b"):
        self.nbuf += 1
        return Buf(f"{name}{self.nbuf}")

    def bufs(self, n, name="b"):
        return [self.buf(name) for _ in range(n)]

    def _deps(self, reads, writes, partial):
        deps = {}

        def add(d):
            for k, v in d.items():
                if k not in deps:
                    deps[k] = v
                elif isinstance(k, str):
                    deps[k] = max(deps[k], v)
                elif v[1] > deps[k][1]:
                    deps[k] = v

        for b in reads:
            add(b.w)
        for b in writes:
            add(b.r)
            if not partial:
                add(b.w)
        return deps

    def _wait(self, eng, deps):
        e = self.engs[eng]
        seen = self.seen[eng]
        for k, v in deps.items():
            if isinstance(k, str):
                if seen.get(k, 0) >= v:
                    continue
                e.wait_ge(self.sems[k], v)
                seen[k] = v
            else:
                if seen.get(k, 0) >= v[1]:
                    continue
                e.wait_ge(v[0], v[1])
                seen[k] = v[1]

    def _mark(self, key, val, reads, writes, partial):
        for b in reads:
            b.r[key] = val
        for b in writes:
            if not partial:
                b.w = {}
            b.w[key] = val
            b.r = {}

    def op(self, eng, fns, reads=(), writes=(), partial=False):
        self._wait(eng, self._deps(reads, writes, partial))
        e = self.engs[eng]
        if not isinstance(fns, (list, tuple)):
            fns = [fns]
        ins = None
        for f in fns:
            ins = f(e)
        self.cnt[eng] += 1
        ins.then_inc(self.sems[eng], 1)
        self._mark(eng, self.cnt[eng], reads, writes, partial)

    def dma(self, eng, out, in_, owner, reads=(), writes=(), partial=False):
        self._wait(eng, self._deps(reads, writes, partial))
        if owner.dsem is None:
            owner.dsem = self.es.enter_context(self.nc.semaphore("d_" + owner.name))
        ins = self.engs[eng].dma_start(out=out, in_=in_)
        owner.dval += 16
        ins.then_inc(owner.dsem, 16)
        self._mark(("d", owner.name), (owner.dsem, owner.dval), reads, writes, partial)

    def fence(self, bufs):
        for b in bufs:
            for n in ("pe", "act", "dve", "pool"):
                if self.cnt[n]:
                    b.r[n] = self.cnt[n]


def bcast_rows(ap, nrows, off, n):
    return bass.AP(ap.tensor, off, [[0, nrows], [1, n]])


def build_nc():
    nc = bass.Bass("TRN2", target_bir_lowering=False)
    dt_in = lambda name, shape: nc.dram_tensor(name, shape, F32, kind="ExternalInput").ap()
    x = dt_in("x", [S, D])
    w_in = dt_in("w_in", [D, 7168])
    w_pa = dt_in("w_pa", [D, D])
    w_pr = dt_in("w_pr", [D, D])
    w_out = dt_in("w_out", [D, D])
    w_up = dt_in("w_up", [D, DFF])
    w_down = dt_in("w_down", [DFF, D])
    w_rg = dt_in("w_rg", [128, 2 * 8 * 128])
    cols_d = dt_in("cols", [128, NCOLS])
    cst_d = dt_in("cst", [128, 512])
    oht_d = dt_in("oht", [33, 383])
    relb = dt_in("rel_bias", [32, 8])
    lamv_d = dt_in("lamv", [1, 256])
    gmix_d = dt_in("gmix", [1, D])
    gmlp_d = dt_in("gmlp", [1, D])
    subg_d = dt_in("subg", [1, 128])
    y = nc.dram_tensor("y", [S, D], F32, kind="ExternalOutput").ap()
    internal = lambda name, shape, dt: nc.dram_tensor(name, shape, dt, kind="Internal").ap()
    vscr = internal("vscr", [8, 383], F32)
    winr_t = internal("winr_t", [32, 128, 8, 128], BF16)
    wpa_t = internal("wpa_t", [8, 128, 8, 128], BF16)
    wpr_t = internal("wpr_t", [8, 128, 8, 128], BF16)
    wup_t = internal("wup_t", [32, 128, 8, 128], BF16)
    wout_t = internal("wout_t", [128, 8, 1024], BF16)
    wdown_t = internal("wdown_t", [128, 32, 1024], BF16)

    with ExitStack() as es:
        k = K(nc, es)
        uid = [0]

        def sb(st, name, shape, dt):
            uid[0] += 1
            return st.enter_context(nc.sbuf_tensor(f"s{uid[0]}_{name}", shape, dt))

        def ps(st, name, shape, dt):
            uid[0] += 1
            return st.enter_context(nc.psum_tensor(f"p{uid[0]}_{name}", shape, dt))

        AT = sb(es, "AT", [128, 8, S], BF16)
        ATb = [k.bufs(NCH, "AT") for _ in range(H)]
        cst = sb(es, "cst", [128, 4, 128], BF16)
        cstb = k.buf("cst")
        ident, bd64, m128, ones = cst[:, 0, :], cst[:, 1, :], cst[:, 2, :], cst[:, 3, :]
        cols = sb(es, "cols", [128, NCOLS], F32)
        colsb = k.buf("cols")
        dcol = sb(es, "dcol", [128, NDC], F32)
        dcolb = k.buf("dcol")
        bias31 = sb(es, "bias31", [128, 8], F32)
        bias31b = k.buf("bias31")
        gmix = sb(es, "gmix", [128, D], F32)
        gmixb = k.buf("gmix")
        gmlp = sb(es, "gmlp", [128, D], F32)
        gmlpb = k.buf("gmlp")
        wrg = sb(es, "wrg", [128, 2, 8, 128], BF16)
        wrgb = k.buf("wrg")
        gsubT = sb(es, "gsubT", [128, 128], F32)
        gsubTb = k.buf("gsubT")

        k.dma("sp", cols[:], cols_d[:], colsb, writes=[colsb])
        k.dma("pool", cst[:].rearrange("p a b -> p (a b)"), cst_d[:], cstb, writes=[cstb])
        k.dma("sp", bias31[:], bcast_rows(relb, 128, 31 * 8, 8), bias31b, writes=[bias31b])
        k.dma("sp", gmix[:], bcast_rows(gmix_d, 128, 0, D), gmixb, writes=[gmixb])
        k.dma("sp", gmlp[:], bcast_rows(gmlp_d, 128, 0, D), gmlpb, writes=[gmlpb])
        k.dma("pool", wrg[:].rearrange("p a c f -> p (a c f)"), w_rg[:], wrgb, writes=[wrgb])
        k.dma("sp", gsubT[:], bcast_rows(subg_d, 128, 0, 128), gsubTb, writes=[gsubTb])
        k.op("dve", lambda e: e.tensor_scalar(out=gsubT[:], in0=gsubT[:], scalar1=0.8, scalar2=None, op0=ALU.mult), reads=[gsubTb], writes=[gsubTb])

        vsb = k.buf("vscr")
        with ExitStack() as s0:
            lamv = sb(s0, "lamv", [128, 256], F32)
            lamvb = k.buf("lamv")
            rb33 = sb(s0, "rb33", [33, 8], F32)
            rb33b = k.buf("rb33")
            oht = sb(s0, "oht", [33, 383], F32)
            ohtb = k.buf("oht")
            vecs = sb(s0, "vecs", [8, 383], F32)
            vecsb = k.buf("vecs")
            tmpc = sb(s0, "tmpc", [128, 64], F32)
            tmpcb = k.buf("tmpc")
            junk = sb(s0, "junk", [128, 64], F32)
            junkb = k.buf("junk")
            vps = ps(s0, "vps", [8, 383], F32)
            vpsb = k.buf("vps")
            k.dma("sp", lamv[:], bcast_rows(lamv_d, 128, 0, 256), lamvb, writes=[lamvb])
            k.op("dve", lambda e: e.memset(rb33[:], NEG), writes=[rb33b])
            k.dma("sp", rb33[0:32, :], relb[:], rb33b, writes=[rb33b])
            k.dma("sp", oht[:], oht_d[:], ohtb, writes=[ohtb])
            k.op("dve", lambda e: e.tensor_tensor(out=dcol[:, D_GK:D_GK + 1], in0=cols[:, C_GQ:C_GQ + 1],
                                                  in1=cols[:, C_GK:C_GK + 1], op=ALU.mult), reads=[colsb], writes=[dcolb])
            k.op("dve", lambda e: e.tensor_scalar(out=dcol[:, D_GSUB:D_GSUB + 1], in0=cols[:, C_SUB:C_SUB + 1],
                                                  scalar1=0.8, scalar2=None, op0=ALU.mult), reads=[colsb], writes=[dcolb], partial=True)
            k.op("dve", lambda e: e.tensor_tensor(out=junk[:, 0:64], in0=lamv[:, 0:64], in1=lamv[:, 64:128], op=ALU.mult),
                 reads=[lamvb], writes=[junkb])
            k.op("dve", lambda e: e.reduce_sum(out=tmpc[:, 0:1], in_=junk[:, 0:64], axis=mybir.AxisListType.X),
                 reads=[junkb], writes=[tmpcb])
            k.op("dve", lambda e: e.tensor_tensor(out=junk[:, 0:64], in0=lamv[:, 128:192], in1=lamv[:, 192:256], op=ALU.mult),
                 reads=[lamvb], writes=[junkb])
            k.op("dve", lambda e: e.reduce_sum(out=tmpc[:, 1:2], in_=junk[:, 0:64], axis=mybir.AxisListType.X),
                 reads=[junkb], writes=[tmpcb], partial=True)
            k.op("act", lambda e: e.activation(out=tmpc[:, 2:4], in_=tmpc[:, 0:2], func=AF.Exp), reads=[tmpcb], writes=[tmpcb])
            k.op("dve", lambda e: e.tensor_tensor(out=tmpc[:, 4:5], in0=tmpc[:, 3:4], in1=tmpc[:, 2:3], op=ALU.subtract),
                 reads=[tmpcb], writes=[tmpcb])
            k.op("dve", lambda e: e.tensor_scalar(out=dcol[:, D_NLAM:D_NLAM + 1], in0=tmpc[:, 4:5], scalar1=-0.2, scalar2=None,
                                                  op0=ALU.add), reads=[tmpcb], writes=[dcolb], partial=True)
            k.op("dve", lambda e: e.tensor_scalar(out=dcol[:, D_HBA:D_HBA + 16], in0=cols[:, C_BA:C_BA + 16], scalar1=0.5,
                                                  scalar2=None, op0=ALU.mult), reads=[colsb], writes=[dcolb], partial=True)
            yv, zv, z2, acc = tmpc[:, 8:16], tmpc[:, 16:24], tmpc[:, 24:32], tmpc[:, 32:40]
            k.op("act", lambda e: e.activation(out=yv, in_=cols[:, C_LAM:C_LAM + 8], func=AF.Exp, scale=-1.0),
                 reads=[colsb], writes=[tmpcb])
            k.op("dve", lambda e: e.tensor_scalar(out=zv, in0=yv, scalar1=2.0, scalar2=None, op0=ALU.add), reads=[tmpcb], writes=[tmpcb])
            k.op("dve", lambda e: e.reciprocal(out=zv, in_=zv), reads=[tmpcb], writes=[tmpcb])
            k.op("dve", lambda e: e.tensor_tensor(out=zv, in0=zv, in1=yv, op=ALU.mult), reads=[tmpcb], writes=[tmpcb])
            k.op("dve", lambda e: e.tensor_tensor(out=z2, in0=zv, in1=zv, op=ALU.mult), reads=[tmpcb], writes=[tmpcb])
            k.op("dve", lambda e: e.memset(acc, 1.0 / 19.0), reads=[tmpcb], writes=[tmpcb])
            for n in (17, 15, 13, 11, 9, 7, 5, 3, 1):
                k.op("dve", lambda e: e.tensor_tensor(out=acc, in0=acc, in1=z2, op=ALU.mult), reads=[tmpcb], writes=[tmpcb])
                k.op("dve", lambda e, n=n: e.tensor_scalar(out=acc, in0=acc, scalar1=1.0 / n, scalar2=None, op0=ALU.add),
                     reads=[tmpcb], writes=[tmpcb])
            k.op("dve", lambda e: e.tensor_tensor(out=acc, in0=acc, in1=zv, op=ALU.mult), reads=[tmpcb], writes=[tmpcb])
            k.op("dve", lambda e: e.tensor_scalar(out=dcol[:, D_M4:D_M4 + 8], in0=acc, scalar1=-8.0, scalar2=None, op0=ALU.mult),
                 reads=[tmpcb], writes=[dcolb], partial=True)
            k.op("dve", lambda e: e.tensor_scalar(out=dcol[:, D_M8:D_M8 + 8], in0=acc, scalar1=-16.0, scalar2=None, op0=ALU.mult),
                 reads=[tmpcb], writes=[dcolb], partial=True)
            k.op("pe", lambda e: e.matmul(vps[:], lhsT=rb33[:], rhs=oht[:], start=True, stop=True), reads=[rb33b, ohtb], writes=[vpsb])
            k.op("dve", lambda e: e.tensor_copy(out=vecs[:], in_=vps[:]), reads=[vpsb], writes=[vecsb])
            k.dma("sp", vscr[:], vecs[:], vsb, reads=[vecsb], writes=[vsb])
            k.op("dve", lambda e: e.memset(junk[:, 0:1], 0.0), reads=[vsb, tmpcb, lamvb, rb33b, ohtb], writes=[junkb])
        setup_fence = dict(k.cnt)

        winrb, wpab, wprb, wupb, woutb, wdownb = (k.buf(n) for n in ("winr", "wpa", "wpr", "wup", "wout", "wdown"))
        conv_jobs = []
        for cc in range(32):
            conv_jobs.append((winr_t[cc], w_in[:, 3072 + 128 * cc:3072 + 128 * (cc + 1)].rearrange("(c p) n -> p c n", p=128), winrb))
        for j in range(8):
            conv_jobs.append((wpa_t[j], w_pa[:, 128 * j:128 * (j + 1)].rearrange("(c p) n -> p c n", p=128), wpab))
            conv_jobs.append((wpr_t[j], w_pr[:, 128 * j:128 * (j + 1)].rearrange("(c p) n -> p c n", p=128), wprb))
        conv_jobs.append((wout_t[:], w_out.rearrange("(c p) n -> p c n", p=128), woutb))
        for f in range(32):
            conv_jobs.append((wup_t[f], w_up[:, 128 * f:128 * (f + 1)].rearrange("(c p) n -> p c n", p=128), wupb))
        for q4 in range(4):
            conv_jobs.append((wdown_t[:, 8 * q4:8 * (q4 + 1), :],
                              w_down[1024 * q4:1024 * (q4 + 1), :].rearrange("(c p) n -> p c n", p=128), wdownb))

        def issue_conv(n):
            for _ in range(n):
                if conv_jobs:
                    o, i, b = conv_jobs.pop(0)
                    k.dma("pool", o, i, b, writes=[b], partial=True)

        def norm_a(st_x, xbuf, gt, gtb, xn, xnb, st, stb, junkt, junktb):
            jb = junktb if isinstance(junktb, list) else [junktb]
            k.op("act", lambda e: e.activation(out=junkt, in_=st_x, func=AF.Square, accum_out=st[:, 0:1]),
                 reads=[xbuf], writes=jb + [stb])
            k.op("act", lambda e: e.activation(out=st[:, 1:2], in_=st[:, 0:1], func=AF.Ln, bias=EPS, scale=1.0 / D),
                 reads=[stb], writes=[stb])
            k.op("act", lambda e: e.activation(out=st[:, 2:3], in_=st[:, 1:2], func=AF.Exp, scale=-0.5), reads=[stb], writes=[stb])
            k.op("dve", lambda e: e.scalar_tensor_tensor(out=xn[:], in0=st_x, scalar=st[:, 2:3], in1=gt[:], op0=ALU.mult, op1=ALU.mult),
                 reads=[xbuf, stb, gtb], writes=[xnb])

        def norm_b(dst_fn, dstb, tp_ps, tpb, xn, xnb, evac_eng):
            k.op("pe", [lambda e, c=c: e.transpose(out=tp_ps[:, c, :], in_=xn[:, 128 * c:128 * (c + 1)], identity=ident) for c in range(8)],
                 reads=[xnb, cstb], writes=[tpb])
            k.op(evac_eng, lambda e: e.tensor_copy(out=dst_fn(), in_=tp_ps[:]) if evac_eng != "act" else e.copy(out=dst_fn(), in_=tp_ps[:]),
                 reads=[tpb], writes=[dstb], partial=True)

        def norm_transpose(st_x, xbuf, gt, gtb, dst_fn, dstb, tp_ps, tpb, xn, xnb, st, stb, junkt, junktb, evac_eng):
            norm_a(st_x, xbuf, gt, gtb, xn, xnb, st, stb, junkt[:] if hasattr(junkt, "shape") and len(junkt.shape) == 2 else junkt, junktb)
            norm_b(dst_fn, dstb, tp_ps, tpb, xn, xnb, evac_eng)

        with ExitStack() as s12:
            hT = sb(s12, "hT", [128, 8, S], BF16)
            hTb = k.bufs(NCH, "hT")
            with ExitStack() as s1:
                xt = [sb(s1, f"xt{i}", [128, D], F32) for i in range(3)]
                xtb = k.bufs(3, "xt")
                xn = [sb(s1, f"xn{i}", [128, D], BF16) for i in range(2)]
                xnb = k.bufs(2, "xn")
                st = [sb(s1, f"st{i}", [128, 4], F32) for i in range(2)]
                stb = k.bufs(2, "st")
                junkt = sb(s1, "junkt", [128, D], BF16)
                junktb = k.buf("junkt")
                tp = [ps(s1, f"tp{i}", [128, 8, 128], BF16) for i in range(2)]
                tpb = k.bufs(2, "tp")
                k.fence(xtb + xnb + stb + [junktb] + tpb)
                issue_conv(0)
                for i in range(NT):
                    s3i, s2i = i % 3, i % 2
                    k.dma("sp", xt[s3i][:], x[128 * i:128 * (i + 1), :], xtb[s3i], writes=[xtb[s3i]])
                    norm_transpose(xt[s3i][:], xtb[s3i], gmix, gmixb,
                                   lambda i=i: hT[:, :, 128 * i:128 * (i + 1)], hTb[i // 4],
                                   tp[s2i], tpb[s2i], xn[s2i], xnb[s2i], st[s2i], stb[s2i], junkt, junktb,
                                   "dve" if i % 2 == 0 else "act")
                k.op("dve", lambda e: e.memset(st[0][:, 3:4], 0.0), reads=xtb + xnb + [junktb] + tpb + stb, writes=[stb[0]])

            with ExitStack() as s2:
                qT = sb(s2, "qT", [128, S], BF16)
                qTb = k.bufs(NCH, "qT")
                kT = sb(s2, "kT", [128, S], BF16)
                kTb = k.bufs(NCH, "kT")
                vv = sb(s2, "vv", [128, NT, 129], BF16)
                vvb = k.bufs(NCH, "vv")
                wqkv = [sb(s2, f"wqkv{i}", [128, 3, 8, 128], BF16) for i in range(2)]
                wqkvb = k.bufs(2, "wqkv")
                Pt = [sb(s2, f"P{i}", [128, 2, 512], BF16) for i in range(3)]
                Ptb = k.bufs(3, "P")
                tmpdb = []
                sq = [sb(s2, f"sq{i}", [128, 512], BF16) for i in range(2)]
                sqb = k.bufs(2, "sq")
                rs = [sb(s2, f"rs{i}", [128, 512], F32) for i in range(2)]
                rsb = k.bufs(2, "rs")
                zz = sb(s2, "zz", [128, 16], F32)
                zzb = k.buf("zz")
                tmpo = sb(s2, "tmpo", [128, 128], F32)
                tmpob = k.buf("tmpo")
                dd = sb(s2, "dd", [128, 4, 128], F32)
                ddb = k.buf("dd")
                abf = sb(s2, "abf", [128, 4, 128], BF16)
                abfb = k.buf("abf")
                junk2 = sb(s2, "junk2", [128, 4, 128], F32)
                junk2b = k.buf("junk2")
                hank = [sb(s2, f"hank{i}", [128, 2, 128], F32) for i in range(2)]
                hankb = k.bufs(2, "hank")
                T01 = [sb(s2, f"T01{i}", [128, 2, 128], F32) for i in range(2)]
                T01b = k.bufs(2, "T01")
                E01 = [sb(s2, f"E01{i}", [128, 2, 128], F32) for i in range(2)]
                E01b = k.bufs(2, "E01")
                nb31 = sb(s2, "nb31", [128, 8], F32)
                nb31b = k.buf("nb31")
                ozc = sb(s2, "ozc", [128, 3, 387], F32)
                ozcb = k.buf("ozc")
                PS = [ps(s2, f"PS{i}", [128, 2, 512], F32) for i in range(2)]
                OZ = ps(s2, "OZ", [128, 3, 512], F32)
                TPp = ps(s2, "TPp", [128, 4, 128], BF16)
                PS.append(OZ)
                PSb = k.bufs(3, "PS")
                OZb = PSb[2]
                TPb = k.buf("TPp")
                k.fence(qTb + kTb + vvb + wqkvb + Ptb + tmpdb + sqb + rsb + [zzb, ddb, tmpob, abfb, junk2b, TPb] + hankb + T01b + E01b + [nb31b, ozcb] + PSb)
                k.op("dve", lambda e: e.memset(vv[:, :, 128:129], 1.0), writes=vvb)
                k.op("dve", lambda e: e.tensor_scalar(out=nb31[:], in0=bias31[:], scalar1=-1.0, scalar2=None, op0=ALU.mult), reads=[bias31b], writes=[nb31b])

                def acc(i, c):
                    a_ = 2 * i + c
                    return OZ[:, a_ // 3, 129 * (a_ % 3):129 * (a_ % 3) + 129]

                def load_head_w(h):
                    sl = h % 2
                    for wi in range(3):
                        c0 = 1024 * wi + 128 * h
                        k.dma("pool", wqkv[sl][:, wi, :, :], w_in[:, c0:c0 + 128].rearrange("(c p) n -> p c n", p=128),
                              wqkvb[sl], writes=[wqkvb[sl]], partial=(wi > 0))
                    k.dma("sp", hank[sl][:, 0, :], bass.AP(vscr.tensor, 383 * h + 128, [[1, 128], [1, 128]]), hankb[sl],
                          reads=[vsb], writes=[hankb[sl]])
                    k.dma("sp", hank[sl][:, 1, :], bass.AP(vscr.tensor, 383 * h, [[1, 128], [1, 128]]), hankb[sl],
                          reads=[vsb], writes=[hankb[sl]], partial=True)

                load_head_w(0)
                gcnt = [0, 0, 0, 0]
                scnt = [0]
                cur_slot = [0]
                for h in range(H):
                    sl = h % 2
                    if h + 1 < H:
                        load_head_w(h + 1)
                    issue_conv(11)
                    k.op("dve", lambda e: e.tensor_copy(out=T01[sl][:], in_=bass.AP(hank[sl], 127, [[256, 128], [128, 2], [-1, 128]])),
                         reads=[hankb[sl]], writes=[T01b[sl]])
                    k.op("act", lambda e: e.activation(out=E01[sl][:], in_=T01[sl][:], func=AF.Exp, bias=nb31[:, h:h + 1], scale=1.0),
                         reads=[T01b[sl], nb31b], writes=[E01b[sl]])
                    def proj_a(wi, t):
                        pi = gcnt[0] % 3
                        gcnt[0] += 1
                        si = gcnt[1] % 2
                        gcnt[1] += 1
                        P_, Pb_ = PS[pi], PSb[pi]
                        k.op("pe", [lambda e, c=c: e.matmul(P_[:, 0, :], lhsT=wqkv[sl][:, wi, c, :], rhs=hT[:, c, 512 * t:512 * (t + 1)],
                                                            start=(c == 0), stop=(c == 7)) for c in range(8)],
                             reads=[wqkvb[sl], hTb[t]], writes=[Pb_])
                        k.op("act", lambda e: e.activation(out=sq[si][:], in_=P_[:, 0, :], func=AF.Square), reads=[Pb_], writes=[sqb[si]])
                        return (wi, t, pi, si)

                    def proj_b(st_):
                        wi, t, pi, si = st_
                        P_, Pb_ = PS[pi], PSb[pi]
                        k.op("pe", lambda e: e.matmul(P_[:, 1, :], lhsT=bd64, rhs=sq[si][:], start=True, stop=True),
                             reads=[sqb[si], cstb], writes=[Pb_], partial=True)
                        k.op("act", lambda e: e.activation(out=rs[si][:], in_=P_[:, 1, :], func=AF.Ln, bias=EPS), reads=[Pb_], writes=[rsb[si]])
                        k.op("act", lambda e: e.activation(out=rs[si][:], in_=rs[si][:], func=AF.Exp, scale=-0.5), reads=[rsb[si]], writes=[rsb[si]])
                        if wi == 0:
                            k.op("dve", lambda e: e.tensor_tensor(out=qT[:, 512 * t:512 * (t + 1)], in0=P_[:, 0, :], in1=rs[si][:], op=ALU.mult),
                                 reads=[Pb_, rsb[si]], writes=[qTb[t]])
                        else:
                            k.op("dve", lambda e: e.scalar_tensor_tensor(out=kT[:, 512 * t:512 * (t + 1)], in0=P_[:, 0, :],
                                                                         scalar=dcol[:, D_GK:D_GK + 1], in1=rs[si][:], op0=ALU.mult, op1=ALU.mult),
                                 reads=[Pb_, rsb[si], dcolb], writes=[kTb[t]])

                    plist = [(wi, t) for wi in range(2) for t in range(NCH)]
                    prev = None
                    for (wi, t) in plist:
                        cur = proj_a(wi, t)
                        if prev is not None:
                            proj_b(prev)
                        prev = cur
                    proj_b(prev)
                    for tg in range(NCH):
                        pi = gcnt[0] % 3
                        gcnt[0] += 1
                        P_, Pb_ = PS[pi], PSb[pi]
                        fl = []
                        for i4 in range(4):
                            tt = 4 * tg + i4
                            for c in range(8):
                                fl.append(lambda e, c=c, i4=i4, tt=tt: e.matmul(P_[:, 0, 128 * i4:128 * (i4 + 1)], lhsT=hT[:, c, 128 * tt:128 * (tt + 1)],
                                                                               rhs=wqkv[sl][:, 2, c, :], start=(c == 0), stop=(c == 7)))
                        k.op("pe", fl, reads=[wqkvb[sl], hTb[tg]], writes=[Pb_])
                        k.op("dve", lambda e: e.tensor_copy(out=vv[:, 4 * tg:4 * tg + 4, 0:128], in_=P_[:, 0, :].rearrange("p (a b) -> p a b", b=128)),
                             reads=[Pb_], writes=[vvb[tg]], partial=True)
                    iters = []
                    for t in range(NCH):
                        full = list(range(0, max(0, 4 * t - 1)))
                        diag = list(range(max(0, 4 * t - 1), 4 * t + 4))
                        order = []
                        if full:
                            order.append(full.pop(0))
                        while full or diag:
                            if diag:
                                order.append(diag.pop(0))
                            if full:
                                order.append(full.pop(0))
                        iters += [(t, j) for j in order]
                    lastj = {}
                    for (t, j) in iters:
                        for i in range(max(0, j - 4 * t), 4):
                            lastj[(t, i)] = j
                    sidx = {}

                    pending = []

                    def emit_qk(n):
                        t, j = iters[n]
                        si = scnt[0] % 2
                        scnt[0] += 1
                        sidx[n] = si
                        c0 = max(0, 128 * (j - 4 * t))
                        k.op("pe", [lambda e, c=c: e.matmul(PS[si][:, c, c0:512], lhsT=kT[64 * c:64 * (c + 1), 128 * j:128 * (j + 1)],
                                                            rhs=qT[64 * c:64 * (c + 1), 512 * t + c0:512 * (t + 1)], start=True, stop=True)
                                    for c in range(2)],
                             reads=[kTb[j // 4], qTb[t]], writes=[PSb[si]])

                    emit_qk(0)
                    emit_qk(1)
                    for n, (t, j) in enumerate(iters):
                        si = sidx[n]
                        cur_slot[0] = si
                        Sp, Spb = PS[si], PSb[si]
                        pi = gcnt[2] % 3
                        gcnt[2] += 1
                        Pp, Ppb = Pt[pi], Ptb[pi]
                        jj = j - 4 * t
                        bcol = bias31[:, h:h + 1]
                        c0 = max(0, 128 * jj)
                        k.op("act", lambda e: e.activation(out=Pp[:, :, c0:512], in_=Sp[:, :, c0:512], func=AF.Exp, bias=bcol, scale=0.125),
                             reads=[Spb, bias31b], writes=[Ppb])
                        if jj >= -1:
                            if jj == -1:
                                cb, toff, nb = 0, 128, 1
                            elif jj == 3:
                                cb, toff, nb = 384, 0, 1
                            else:
                                cb, toff, nb = 128 * jj, 0, 2
                            k.op("dve", lambda e: e.tensor_tensor(out=Pp[:, :, cb:cb + 128 * nb], in0=Pp[:, :, cb:cb + 128 * nb],
                                                                  in1=bass.AP(E01[sl], toff, [[256, 128], [0, 2], [1, 128 * nb]]), op=ALU.mult),
                                 reads=[Ppb, E01b[sl]], writes=[Ppb])
                        if n + 2 < len(iters):
                            emit_qk(n + 2)
                        first = (n == 0 or iters[n - 1][0] != t)
                        fl = []
                        for i in range(max(0, jj), 4):
                            for c in range(2):
                                fl.append(lambda e, i=i, c=c: e.matmul(acc(i, c), lhsT=Pp[:, c, 128 * i:128 * (i + 1)], rhs=vv[:, j, :],
                                                                       start=(first and (2 * i + c) % 3 == 0), stop=(j == lastj[(t, i)]),
                                                                       skip_group_check=True))
                        for _ in range(NDUMMY):
                            fl.append(lambda e: e.matmul(OZ[:, 2, 258:512], lhsT=ident, rhs=kT[:, 0:254], start=False, stop=False, skip_group_check=True))
                        k.op("pe", fl, reads=[Ppb, vvb[j // 4], cstb], writes=[OZb], partial=not first)
                        last = (n == len(iters) - 1 or iters[n + 1][0] != t)
                        if last:
                            k.op("dve", lambda e: e.tensor_copy(out=ozc[:], in_=OZ[:, :, 0:387]), reads=[OZb], writes=[ozcb])
                            zcols = bass.AP(ozc, 128, [[1161, 128], [387, 3], [129, 3]])

                            def accs(i, c):
                                a_ = 2 * i + c
                                return ozc[:, a_ // 3, 129 * (a_ % 3):129 * (a_ % 3) + 128]

                            k.op("dve", lambda e: e.reciprocal(out=zz[:, 0:9].rearrange("p (a b) -> p a b", b=3), in_=zcols), reads=[ozcb], writes=[zzb])
                            k.op("dve", lambda e: e.tensor_scalar(out=zz[:, 0:8].rearrange("p (i c) -> p i c", c=2)[:, :, 1:2],
                                                                  in0=zz[:, 0:8].rearrange("p (i c) -> p i c", c=2)[:, :, 1:2],
                                                                  scalar1=dcol[:, D_NLAM:D_NLAM + 1], scalar2=None, op0=ALU.mult),
                                 reads=[zzb, dcolb], writes=[zzb])
                            for i in range(4):
                                k.op("dve", lambda e, i=i: e.tensor_scalar(out=tmpo[:], in0=accs(i, 0), scalar1=zz[:, 2 * i:2 * i + 1], scalar2=None,
                                                                           op0=ALU.mult), reads=[ozcb, zzb], writes=[tmpob])
                                k.op("dve", lambda e, i=i: e.scalar_tensor_tensor(out=dd[:, i, :], in0=accs(i, 1), scalar=zz[:, 2 * i + 1:2 * i + 2],
                                                                                  in1=tmpo[:], op0=ALU.mult, op1=ALU.add),
                                     reads=[ozcb, zzb, tmpob], writes=[ddb], partial=(i > 0))

                            def part2a(t=t):
                                k.op("dve", lambda e: e.tensor_tensor(out=junk2[:], in0=dd[:], in1=dd[:], op=ALU.mult), reads=[ddb], writes=[junk2b])
                                k.op("dve", lambda e: e.reduce_sum(out=zz[:, 8:12], in_=junk2[:], axis=mybir.AxisListType.X), reads=[junk2b], writes=[zzb], partial=True)
                                k.op("act", lambda e: e.activation(out=zz[:, 12:16], in_=zz[:, 8:12], func=AF.Ln, bias=EPS, scale=1.0 / 128), reads=[zzb], writes=[zzb])
                                k.op("act", lambda e: e.activation(out=zz[:, 12:16], in_=zz[:, 12:16], func=AF.Exp, scale=-0.5), reads=[zzb], writes=[zzb])
                                for i in range(4):
                                    k.op("dve", lambda e, i=i: e.scalar_tensor_tensor(out=abf[:, i, :], in0=dd[:, i, :], scalar=zz[:, 12 + i:13 + i], in1=gsubT[:],
                                                                                      op0=ALU.mult, op1=ALU.mult),
                                         reads=[ddb, zzb, gsubTb], writes=[abfb], partial=(i > 0))

                            def part2b(t=t):
                                k.op("pe", [lambda e, i=i: e.transpose(out=TPp[:, i, :], in_=abf[:, i, :], identity=ident) for i in range(4)],
                                     reads=[abfb, cstb], writes=[TPb])
                                k.op("dve", lambda e: e.tensor_copy(out=AT[:, h, 512 * t:512 * (t + 1)], in_=TPp[:].rearrange("p a b -> p (a b)")),
                                     reads=[TPb], writes=[ATb[h][t]])

                            pending.append((n + 3, part2a))
                            pending.append((n + 7, part2b))
                        while pending and (pending[0][0] <= n or n == len(iters) - 1):
                            pending.pop(0)[1]()
                issue_conv(1000)
                k.op("dve", lambda e: e.memset(dd[:, 0:1], 0.0),
                     reads=qTb + kTb + vvb + wqkvb + Ptb + tmpdb + sqb + rsb + [zzb, tmpob, abfb, junk2b, TPb, ozcb, nb31b] + hankb + T01b + E01b + PSb + hTb, writes=[ddb])

        with ExitStack() as s3:
            xres = [sb(s3, f"xres{i}", [128, D], F32) for i in range(4)]
            xresb = k.bufs(4, "xres")
            mgT = sb(s3, "mgT", [128, 8, 512], BF16)
            mgTb = k.bufs(8, "mgT")
            RT = sb(s3, "RT", [128, 8, 512], BF16)
            RTb = k.bufs(8, "RT")
            xn = sb(s3, "xn3", [128, D], BF16)
            xnb = k.buf("xn3")
            junkt = mgT[:, 0:2, :].rearrange("p a b -> p (a b)")
            junktb = [mgTb[0], mgTb[1]]
            st = sb(s3, "st3", [128, 4], F32)
            stb = k.buf("st3")
            carry = sb(s3, "carry", [128, 8, 4], F32)
            carryb = k.bufs(8, "carry")
            wA = [sb(s3, f"wA{i}", [128, 8, 128], BF16) for i in range(7)]
            wAb = k.bufs(7, "wA")
            wO = sb(s3, "wO", [128, 8, 512], BF16)
            wOb = k.buf("wO")
            wD = [sb(s3, f"wD{i}", [128, 4, 512], BF16) for i in range(2)]
            wDb = k.bufs(2, "wD")
            tp = ps(s3, "tp3", [128, 8, 128], BF16)
            tpb = k.buf("tp3")
            G = [ps(s3, f"G{i}", [128, 512], F32) for i in range(7)]
            Gb = k.bufs(7, "G")
            k.fence(xresb + mgTb + RTb + [xnb, stb] + carryb + wAb + [wOb] + wDb + [tpb] + Gb)
            k.op("dve", lambda e: e.memset(carry[:], 0.0), writes=carryb)
            wa_i = [0]

            def load_wA(src, srcb):
                i = wa_i[0] % 7
                wa_i[0] += 1
                k.dma("sp", wA[i][:], src, wAb[i], reads=[srcb], writes=[wAb[i]])
                return wA[i], wAb[i]

            hTc = sb(s3, "hTc", [128, 8, 512], BF16)
            hTcb = k.bufs(1, "hTc")
            xs = [sb(s3, f"xs{i}", [128, D], F32) for i in range(1)]
            xsb = k.bufs(1, "xs")
            k.fence(hTcb + xsb)
            for t in range(NCH):
                tok0 = 512 * t
                with ExitStack() as sa:
                    tnames = ["xr", "xc", "xcb16", "xg", "tr", "ti", "tg", "aa", "a2", "hs"]
                    TT = [{}, {}]
                    BB = [{}, {}]
                    for nm in tnames:
                        single = nm in ("xr", "tr", "hs")
                        for par in range(2):
                            if single and par == 1:
                                TT[1][nm], BB[1][nm] = TT[0][nm], BB[0][nm]
                                continue
                            TT[par][nm] = sb(sa, nm, [128, 515 if nm == "xr" else 512], BF16 if nm == "xcb16" else F32)
                            BB[par][nm] = k.buf(nm)
                    ta = sb(sa, "ta", [128, 512], F32)
                    m2 = sb(sa, "m2", [128, 512], F32)
                    tnP = sb(sa, "tnP", [128, 8, 512], BF16)
                    m1P = sb(sa, "m1P", [128, 8, 512], BF16)
                    B = {n: k.buf(n) for n in ("ta", "m2")}
                    tnPb = k.bufs(8, "tnP")
                    m1Pb = k.bufs(8, "m1P")
                    allb = list(B.values()) + list(BB[0].values()) + list(BB[1].values()) + tnPb + m1Pb
                    k.fence(allb)
                    for i in range(4):
                        k.dma("sp", xres[i][:], x[tok0 + 128 * i:tok0 + 128 * (i + 1), :], xresb[i], writes=[xresb[i]])
                    if t == 0:
                        for i in range(4):
                            norm_transpose(xres[i][:], xresb[i], gmix, gmixb, lambda i=i: hTc[:, :, 128 * i:128 * (i + 1)], hTcb[0],
                                           tp, tpb, xn, xnb, st, stb, junkt, junktb, "dve")

                    def rnn_s1a(c):
                        T_, B_ = TT[c % 2], BB[c % 2]
                        xr, xc, xcb16, xg, tg_ = T_["xr"], T_["xc"], T_["xcb16"], T_["xg"], T_["tg"]
                        wx, wxb = load_wA(winr_t[c], winrb)
                        wg, wgb = load_wA(winr_t[8 + c], winrb)
                        k.op("pe", [lambda e, d=d: e.matmul(G[0][:], lhsT=wx[:, d, :], rhs=hTc[:, d, :], start=(d == 0), stop=(d == 7)) for d in range(8)],
                             reads=[wxb, hTcb[0]], writes=[Gb[0]])
                        k.op("pe", [lambda e, d=d: e.matmul(G[1][:], lhsT=wg[:, d, :], rhs=hTc[:, d, :], start=(d == 0), stop=(d == 7)) for d in range(8)],
                             reads=[wgb, hTcb[0]], writes=[Gb[1]])
                        k.op("act", lambda e: e.copy(out=xr[:, 3:515], in_=G[0][:]), reads=[Gb[0]], writes=[B_["xr"]])
                        k.op("act", lambda e: e.copy(out=xg[:], in_=G[1][:]), reads=[Gb[1]], writes=[B_["xg"]])
                        k.op("pool", lambda e: e.tensor_tensor(out=tg_[:], in0=xg[:], in1=xg[:], op=ALU.mult), reads=[B_["xg"]], writes=[B_["tg"]])
                        k.op("pool", lambda e: e.tensor_scalar(out=tg_[:], in0=tg_[:], scalar1=0.044715, scalar2=1.0, op0=ALU.mult, op1=ALU.add),
                             reads=[B_["tg"]], writes=[B_["tg"]])
                        k.op("pool", lambda e: e.tensor_tensor(out=tg_[:], in0=tg_[:], in1=xg[:], op=ALU.mult), reads=[B_["tg"], B_["xg"]], writes=[B_["tg"]])
                        k.op("dve", lambda e: e.tensor_copy(out=xr[:, 0:3], in_=carry[:, c, 0:3]), reads=[carryb[c]], writes=[B_["xr"]], partial=True)
                        cw = lambda tap: cols[:, C_CW + 8 * tap + c:C_CW + 8 * tap + c + 1]
                        k.op("dve", lambda e: e.tensor_scalar(out=xc[:], in0=xr[:, 0:512], scalar1=cw(0), scalar2=cols[:, C_CB + c:C_CB + c + 1],
                                                              op0=ALU.mult, op1=ALU.add), reads=[B_["xr"], colsb], writes=[B_["xc"]])
                        for tap in (1, 2, 3):
                            k.op("dve", lambda e, tap=tap: e.scalar_tensor_tensor(out=xc[:], in0=xr[:, tap:tap + 512], scalar=cw(tap), in1=xc[:],
                                                                                  op0=ALU.mult, op1=ALU.add), reads=[B_["xr"], B_["xc"], colsb], writes=[B_["xc"]])
                        k.op("dve", lambda e: e.tensor_copy(out=carry[:, c, 0:3], in_=xr[:, 512:515]), reads=[B_["xr"]], writes=[carryb[c]])
                        k.op("pool", lambda e: e.tensor_copy(out=xcb16[:], in_=xc[:]), reads=[B_["xc"]], writes=[B_["xcb16"]])

                    def rnn_s2(c):
                        T_, B_ = TT[c % 2], BB[c % 2]
                        xcb16, xg, tr, ti_, tg_, aa, a2 = T_["xcb16"], T_["xg"], T_["tr"], T_["ti"], T_["tg"], T_["aa"], T_["a2"]
                        k.op("pe", lambda e: e.matmul(G[2][:], lhsT=wrg[:, 0, c, :], rhs=xcb16[:], start=True, stop=True),
                             reads=[wrgb, B_["xcb16"]], writes=[Gb[2]])
                        k.op("pe", lambda e: e.matmul(G[3][:], lhsT=wrg[:, 1, c, :], rhs=xcb16[:], start=True, stop=True),
                             reads=[wrgb, B_["xcb16"]], writes=[Gb[3]])
                        k.op("act", lambda e: e.activation(out=tr[:], in_=G[2][:], func=AF.Tanh, bias=dcol[:, D_HBA + c:D_HBA + c + 1], scale=0.5),
                             reads=[Gb[2], dcolb], writes=[B_["tr"]])
                        k.op("act", lambda e: e.activation(out=ti_[:], in_=G[3][:], func=AF.Tanh, bias=dcol[:, D_HBX + c:D_HBX + c + 1], scale=0.5),
                             reads=[Gb[3], dcolb], writes=[B_["ti"]])
                        k.op("act", lambda e: e.activation(out=tg_[:], in_=tg_[:], func=AF.Tanh, scale=math.sqrt(2.0 / math.pi)),
                             reads=[B_["tg"]], writes=[B_["tg"]])
                        k.op("act", lambda e: e.activation(out=aa[:], in_=tr[:], func=AF.Exp, bias=dcol[:, D_M4 + c:D_M4 + c + 1],
                                                           scale=dcol[:, D_M4 + c:D_M4 + c + 1]), reads=[B_["tr"], dcolb], writes=[B_["aa"]])
                        k.op("act", lambda e: e.activation(out=a2[:], in_=tr[:], func=AF.Exp, bias=dcol[:, D_M8 + c:D_M8 + c + 1],
                                                           scale=dcol[:, D_M8 + c:D_M8 + c + 1]), reads=[B_["tr"], dcolb], writes=[B_["a2"]])
                        k.op("act", lambda e: e.activation(out=a2[:], in_=a2[:], func=AF.Sqrt, bias=1.0, scale=-1.0), reads=[B_["a2"]], writes=[B_["a2"]])

                    def rnn_s3(c):
                        T_, B_ = TT[c % 2], BB[c % 2]
                        xc, xg, ti_, tg_, aa, a2, hs = T_["xc"], T_["xg"], T_["ti"], T_["tg"], T_["aa"], T_["a2"], T_["hs"]
                        k.op("dve", lambda e: e.scalar_tensor_tensor(out=ti_[:], in0=ti_[:], scalar=1.0, in1=xc[:], op0=ALU.add, op1=ALU.mult),
                             reads=[B_["ti"], B_["xc"]], writes=[B_["ti"]])
                        k.op("dve", lambda e: e.tensor_tensor(out=ti_[:], in0=ti_[:], in1=a2[:], op=ALU.mult), reads=[B_["ti"], B_["a2"]], writes=[B_["ti"]])
                        k.op("dve", lambda e: e.tensor_tensor_scan(out=hs[:], data0=aa[:], data1=ti_[:], initial=carry[:, c, 3:4], op0=ALU.mult, op1=ALU.add),
                             reads=[B_["aa"], B_["ti"], carryb[c]], writes=[B_["hs"]])
                        k.op("dve", lambda e: e.tensor_copy(out=carry[:, c, 3:4], in_=hs[:, 511:512]), reads=[B_["hs"]], writes=[carryb[c]])
                        k.op("dve", lambda e: e.scalar_tensor_tensor(out=tg_[:], in0=tg_[:], scalar=1.0, in1=xg[:], op0=ALU.add, op1=ALU.mult),
                             reads=[B_["tg"], B_["xg"]], writes=[B_["tg"]])
                        k.op("dve", lambda e: e.scalar_tensor_tensor(out=RT[:, c, :], in0=hs[:], scalar=0.25, in1=tg_[:], op0=ALU.mult, op1=ALU.mult),
                             reads=[B_["hs"], B_["tg"]], writes=[RTb[c]])

                    def merge_early(j):
                        wga_s, wga_b = load_wA(winr_t[16 + j], winrb)
                        wgn_s, wgn_b = load_wA(winr_t[24 + j], winrb)
                        wpa_s, wpa_b = load_wA(wpa_t[j], wpab)
                        k.op("pe", [lambda e, d=d: e.matmul(G[4][:], lhsT=wga_s[:, d, :], rhs=hTc[:, d, :], start=(d == 0), stop=(d == 7)) for d in range(8)],
                             reads=[wga_b, hTcb[0]], writes=[Gb[4]])
                        k.op("pe", [lambda e, d=d: e.matmul(G[5][:], lhsT=wgn_s[:, d, :], rhs=hTc[:, d, :], start=(d == 0), stop=(d == 7)) for d in range(8)],
                             reads=[wgn_b, hTcb[0]], writes=[Gb[5]])
                        k.op("pe", [lambda e, d=d: e.matmul(G[6][:], lhsT=wpa_s[:, d, :], rhs=AT[:, d, tok0:tok0 + 512], start=(d == 0), stop=(d == 7)) for d in range(8)],
                             reads=[wpa_b] + [ATb[d][t] for d in range(8)], writes=[Gb[6]])
                        k.op("act", lambda e: e.activation(out=ta[:], in_=G[4][:], func=AF.Tanh, scale=0.5), reads=[Gb[4]], writes=[B["ta"]])
                        k.op("act", lambda e: e.activation(out=tnP[:, j, :], in_=G[5][:], func=AF.Tanh, scale=0.5), reads=[Gb[5]], writes=[tnPb[j]])
                        k.op("dve", lambda e: e.scalar_tensor_tensor(out=m1P[:, j, :], in0=ta[:], scalar=1.0, in1=G[6][:], op0=ALU.add, op1=ALU.mult),
                             reads=[B["ta"], Gb[6]], writes=[m1Pb[j]])

                    for s_ in range(10):
                        if 0 <= s_ - 2 < 8:
                            rnn_s3(s_ - 2)
                        if s_ < 8:
                            rnn_s1a(s_)
                        if 0 <= s_ - 1 < 8:
                            rnn_s2(s_ - 1)
                        if 0 <= s_ - 2 < 8:
                            merge_early(s_ - 2)
                    for j in range(8):
                        wpr_s, wpr_b = load_wA(wpr_t[j], wprb)
                        g = j % 2
                        k.op("pe", [lambda e, d=d: e.matmul(G[g][:], lhsT=wpr_s[:, d, :], rhs=RT[:, d, :], start=(d == 0), stop=(d == 7)) for d in range(8)],
                             reads=[wpr_b] + RTb, writes=[Gb[g]])
                        k.op("dve", lambda e: e.scalar_tensor_tensor(out=m2[:], in0=tnP[:, j, :], scalar=1.0, in1=G[g][:], op0=ALU.add, op1=ALU.mult),
                             reads=[tnPb[j], Gb[g]], writes=[B["m2"]])
                        k.op("dve", lambda e: e.tensor_tensor(out=mgT[:, j, :], in0=m2[:], in1=m1P[:, j, :], op=ALU.add), reads=[B["m2"], m1Pb[j]], writes=[mgTb[j]])
                    for half in range(2):
                        k.dma("sp", wO[:], wout_t[:, :, 512 * half:512 * (half + 1)], wOb, reads=[woutb], writes=[wOb])
                        for i in range(4):
                            g = 4 + (2 * half + i) % 2
                            k.op("pe", [lambda e, d=d: e.matmul(G[g][:], lhsT=mgT[:, d, 128 * i:128 * (i + 1)], rhs=wO[:, d, :], start=(d == 0), stop=(d == 7)) for d in range(8)],
                                 reads=[wOb] + mgTb, writes=[Gb[g]])
                            k.op("dve", lambda e: e.scalar_tensor_tensor(out=xres[i][:, 512 * half:512 * (half + 1)], in0=G[g][:], scalar=0.5,
                                                                         in1=xres[i][:, 512 * half:512 * (half + 1)], op0=ALU.mult, op1=ALU.add),
                                 reads=[Gb[g], xresb[i]], writes=[xresb[i]])
                    for i in range(4):
                        norm_transpose(xres[i][:], xresb[i], gmlp, gmlpb, lambda i=i: RT[:, :, 128 * i:128 * (i + 1)], RTb[i],
                                       tp, tpb, xn, xnb, st, stb, junkt, junktb, "dve")
                    k.op("dve", lambda e: e.memset(st[:, 3:4], 0.0), reads=RTb[0:4] + allb, writes=RTb[4:8] + [stb])
                with ExitStack() as sm:
                    uT = sb(sm, "uT", [128, 32, 512], BF16)
                    uTb = k.bufs(32, "uT")
                    rl = [sb(sm, f"rl{i}", [128, 512], F32) for i in range(2)]
                    rlb = k.bufs(2, "rl")
                    k.fence(uTb + rlb)
                    for f in range(32):
                        wu, wub = load_wA(wup_t[f], wupb)
                        g = 4 + f % 3
                        k.op("pe", [lambda e, d=d: e.matmul(G[g][:], lhsT=wu[:, d, :], rhs=RT[:, d, :], start=(d == 0), stop=(d == 7)) for d in range(8)],
                             reads=[wub] + RTb, writes=[Gb[g]])
                        k.op("act", lambda e: e.activation(out=rl[f % 2][:], in_=G[g][:], func=AF.Relu), reads=[Gb[g]], writes=[rlb[f % 2]])
                        k.op("pool", lambda e: e.tensor_tensor(out=uT[:, f, :], in0=rl[f % 2][:], in1=rl[f % 2][:], op=ALU.mult),
                             reads=[rlb[f % 2]], writes=[uTb[f]])
                    def pf_a(i):
                        k.dma("sp", xs[0][:], x[tok0 + 512 + 128 * i:tok0 + 512 + 128 * (i + 1), :], xsb[0], writes=[xsb[0]])
                        norm_a(xs[0][:], xsb[0], gmix, gmixb, xn, xnb, st, stb, junkt, junktb)

                    def pf_b(i):
                        norm_b(lambda i=i: hTc[:, :, 128 * i:128 * (i + 1)], hTcb[0], tp, tpb, xn, xnb, "dve")

                    for half in range(2):
                        for f4 in range(8):
                            g_ = half * 8 + f4
                            if t + 1 < NCH and g_ % 2 == 0 and g_ <= 8:
                                if g_ >= 2:
                                    pf_b(g_ // 2 - 1)
                                if g_ <= 6:
                                    pf_a(g_ // 2)
                            wi = (half * 8 + f4) % 2
                            k.dma("sp", wD[wi][:], wdown_t[:, 4 * f4:4 * f4 + 4, 512 * half:512 * (half + 1)], wDb[wi], reads=[wdownb], writes=[wDb[wi]])
                            for fi in range(4):
                                f = 4 * f4 + fi
                                k.op("pe", [lambda e, i=i: e.matmul(G[i][:], lhsT=uT[:, f, 128 * i:128 * (i + 1)], rhs=wD[wi][:, fi, :],
                                                                    start=(f == 0), stop=(f == 31)) for i in range(4)],
                                     reads=[wDb[wi], uTb[f]], writes=Gb[0:4], partial=(f != 0))
                        for i in range(4):
                            k.op("dve", lambda e: e.tensor_tensor(out=xres[i][:, 512 * half:512 * (half + 1)], in0=G[i][:],
                                                                  in1=xres[i][:, 512 * half:512 * (half + 1)], op=ALU.add),
                                 reads=[Gb[i], xresb[i]], writes=[xresb[i]])
                    for i in range(4):
                        k.dma("sp", y[tok0 + 128 * i:tok0 + 128 * (i + 1), :], xres[i][:], xresb[i], reads=[xresb[i]])
                    k.op("dve", lambda e: e.memset(st[:, 3:4], 0.0), reads=uTb + rlb, writes=[stb])
            k._wait("sp", {kk: vv_ for b in xresb for kk, vv_ in b.r.items()})
            k.op("dve", lambda e: e.memset(st[:, 3:4], 0.0), reads=xresb + mgTb + RTb + wAb + [wOb] + wDb + [tpb] + Gb + carryb + hTcb + xsb, writes=[stb])
    import os
    if os.environ.get('KDEBUG'):
        print('counts', k.cnt, 'ninstr', nc.n_instructions if not callable(nc.n_instructions) else nc.n_instructions())
    return nc


_NC_CACHE = {}


def _host_consts():
    cst = np.zeros((128, 4, 128), np.float32)
    cst[:, 0, :] = np.eye(128, dtype=np.float32)
    bd = np.zeros((128, 128), np.float32)
    bd[:64, :64] = 1.0 / 64
    bd[64:, 64:] = 1.0 / 64
    cst[:, 1, :] = bd
    cst[:, 2, :] = 1.0 / 128
    cst[:, 3, :] = 1.0
    oht = np.zeros((33, 383), np.float32)
    for m in range(383):
        n = 255 - m
        if n < 0:
            b = 32
        elif n < 16:
            b = n
        else:
            nf = np.float32(n)
            b = 16 + int(np.float32(np.log(nf / np.float32(16)) / np.float32(math.log(128 / 16)) * np.float32(16)))
            b = min(b, 31)
        oht[b, m] = 1.0
    return cst.reshape(128, 512), oht


def kernel(**inputs):
    f = lambda name: np.ascontiguousarray(np.asarray(inputs[name], dtype=np.float32))
    x = f("x")
    col = lambda v: np.ascontiguousarray(v.reshape(-1, 128).T)
    cols = np.zeros((128, NCOLS), np.float32)
    cols[:, C_GQ] = np.tile(f("q_norm_g")[0], 2)
    cols[:, C_GK] = np.tile(f("k_norm_g")[0], 2)
    cols[:, C_SUB] = f("subln_g")[0]
    cw = f("conv_w")[0]
    for tap in range(4):
        cols[:, C_CW + 8 * tap:C_CW + 8 * tap + 8] = col(cw[tap])
    cols[:, C_CB:C_CB + 8] = col(f("conv_b")[0])
    cols[:, C_BA:C_BA + 8] = col(f("b_rg_a")[0].reshape(-1))
    cols[:, C_BX:C_BX + 8] = col(f("b_rg_x")[0].reshape(-1))
    cols[:, C_LAM:C_LAM + 8] = col(f("lru_lambda")[0])
    wrg = np.stack([f("w_rg_a")[0], f("w_rg_x")[0]], axis=0)
    wrg = np.ascontiguousarray(wrg.transpose(2, 0, 1, 3).reshape(128, 2 * 8 * 128))
    lamv = np.concatenate([f("lambda_q1")[0], f("lambda_k1")[0], f("lambda_q2")[0], f("lambda_k2")[0]]).reshape(1, 256)
    cst, oht = _host_consts()
    shared = {
        "w_in": f("w_in")[0], "w_pa": f("w_proj_attn")[0], "w_pr": f("w_proj_rnn")[0], "w_out": f("w_out")[0],
        "w_up": f("w_up")[0], "w_down": f("w_down")[0], "w_rg": wrg, "cols": cols, "cst": cst, "oht": oht,
        "rel_bias": f("rel_bias"), "lamv": np.ascontiguousarray(lamv), "gmix": f("norm_mix_g"), "gmlp": f("norm_mlp_g"), "subg": f("subln_g"),
    }
    if "nc" not in _NC_CACHE:
        _NC_CACHE["nc"] = build_nc()
    nc = _NC_CACHE["nc"]
    in_maps = [dict(shared, x=np.ascontiguousarray(x[b])) for b in range(8)]
    res = run_bass_kernel_spmd(nc, in_maps, core_ids=list(range(8)))
    return np.stack([np.asarray(r["y"], dtype=np.float32) for r in res.results], axis=0)
```

```python
import math
from contextlib import ExitStack
import numpy as np
import concourse.bass as bass
import concourse.mybir as mybir
from concourse.bass_utils import run_bass_kernel_spmd

F32 = mybir.dt.float32
BF16 = mybir.dt.bfloat16
AF = mybir.ActivationFunctionType
ALU = mybir.AluOpType

S = 4096
D = 1024
NT = S // 128
NCH = S // 512
H = 8
DFF = 4096
EPS = 1e-6
NEG = -30000.0
NDUMMY = 2
C_GQ, C_GK, C_SUB, C_CW, C_CB, C_BA, C_BX, C_LAM, NCOLS = 0, 1, 2, 3, 35, 43, 51, 59, 67
D_GK, D_GSUB, D_NLAM, D_HBA, D_HBX, D_M4, D_M8, NDC = 0, 1, 2, 3, 11, 19, 27, 35


class Buf:
    def __init__(self, name):
        self.name = name
        self.w = {}
        self.r = {}
        self.dsem = None
        self.dval = 0


class K:
    def __init__(self, nc, es):
        self.nc = nc
        self.es = es
        self.engs = {"pe": nc.tensor, "act": nc.scalar, "dve": nc.vector, "pool": nc.gpsimd, "sp": nc.sync}
        self.sems = {n: es.enter_context(nc.semaphore("s_" + n)) for n in self.engs}
        self.cnt = {n: 0 for n in self.engs}
        self.seen = {n: {} for n in self.engs}
        self.nbuf = 0
        self.stores = []

    def buf(self, name="b"):
        self.nbuf += 1
        return Buf(f"{name}{self.nbuf}")

    def bufs(self, n, name="b"):
        return [self.buf(name) for _ in range(n)]

    def _deps(self, reads, writes, partial):
        deps = {}

        def add(d):
            for k, v in d.items():
                if k not in deps:
                    deps[k] = v
                elif isinstance(k, str):
                    deps[k] = max(deps[k], v)
                elif v[1] > deps[k][1]:
                    deps[k] = v

        for b in reads:
            add(b.w)
        for b in writes:
            add(b.r)
            if not partial:
                add(b.w)
        return deps

    def _wait(self, eng, deps):
        e = self.engs[eng]
        seen = self.seen[eng]
        for k, v in deps.items():
            if isinstance(k, str):
                if seen.get(k, 0) >= v:
                    continue
                e.wait_ge(self.sems[k], v)
                seen[k] = v
            else:
                if seen.get(k, 0) >= v[1]:
                    continue
                e.wait_ge(v[0], v[1])
                seen[k] = v[1]

    def _mark(self, key, val, reads, writes, partial):
        for b in reads:
            b.r[key] = val
        for b in writes:
            if not partial:
                b.w = {}
            b.w[key] = val
            b.r = {}

    def op(self, eng, fns, reads=(), writes=(), partial=False):
        self._wait(eng, self._deps(reads, writes, partial))
        e = self.engs[eng]
        if not isinstance(fns, (list, tuple)):
            fns = [fns]
        ins = None
        for f in fns:
            ins = f(e)
        self.cnt[eng] += 1
        ins.then_inc(self.sems[eng], 1)
        self._mark(eng, self.cnt[eng], reads, writes, partial)

    def dma(self, eng, out, in_, owner, reads=(), writes=(), partial=False):
        self._wait(eng, self._deps(reads, writes, partial))
        if owner.dsem is None:
            owner.dsem = self.es.enter_context(self.nc.semaphore("d_" + owner.name))
        ins = self.engs[eng].dma_start(out=out, in_=in_)
        owner.dval += 16
        ins.then_inc(owner.dsem, 16)
        self._mark(("d", owner.name), (owner.dsem, owner.dval), reads, writes, partial)

    def fence(self, bufs):
        for b in bufs:
            for n in ("pe", "act", "dve", "pool"):
                if self.cnt[n]:
                    b.r[n] = self.cnt[n]


def bcast_rows(ap, nrows, off, n):
    return bass.AP(ap.tensor, off, [[0, nrows], [1, n]])


def build_nc():
    nc = bass.Bass("TRN2", target_bir_lowering=False)
    dt_in = lambda name, shape: nc.dram_tensor(name, shape, F32, kind="ExternalInput").ap()
    x = dt_in("x", [S, D])
    w_in = dt_in("w_in", [D, 7168])
    w_pa = dt_in("w_pa", [D, D])
    w_pr = dt_in("w_pr", [D, D])
    w_out = dt_in("w_out", [D, D])
    w_up = dt_in("w_up", [D, DFF])
    w_down = dt_in("w_down", [DFF, D])
    w_rg = dt_in("w_rg", [128, 2 * 8 * 128])
    cols_d = dt_in("cols", [128, NCOLS])
    cst_d = dt_in("cst", [128, 512])
    oht_d = dt_in("oht", [33, 383])
    relb = dt_in("rel_bias", [32, 8])
    lamv_d = dt_in("lamv", [1, 256])
    gmix_d = dt_in("gmix", [1, D])
    gmlp_d = dt_in("gmlp", [1, D])
    subg_d = dt_in("subg", [1, 128])
    y = nc.dram_tensor("y", [S, D], F32, kind="ExternalOutput").ap()
    internal = lambda name, shape, dt: nc.dram_tensor(name, shape, dt, kind="Internal").ap()
    vscr = internal("vscr", [8, 383], F32)
    winr_t = internal("winr_t", [32, 128, 8, 128], BF16)
    wpa_t = internal("wpa_t", [8, 128, 8, 128], BF16)
    wpr_t = internal("wpr_t", [8, 128, 8, 128], BF16)
    wup_t = internal("wup_t", [32, 128, 8, 128], BF16)
    wout_t = internal("wout_t", [128, 8, 1024], BF16)
    wdown_t = internal("wdown_t", [128, 32, 1024], BF16)

    with ExitStack() as es:
        k = K(nc, es)
        uid = [0]

        def sb(st, name, shape, dt):
            uid[0] += 1
            return st.enter_context(nc.sbuf_tensor(f"s{uid[0]}_{name}", shape, dt))

        def ps(st, name, shape, dt):
            uid[0] += 1
            return st.enter_context(nc.psum_tensor(f"p{uid[0]}_{name}", shape, dt))

        AT = sb(es, "AT", [128, 8, S], BF16)
        ATb = [k.bufs(NCH, "AT") for _ in range(H)]
        cst = sb(es, "cst", [128, 4, 128], BF16)
        cstb = k.buf("cst")
        ident, bd64, m128, ones = cst[:, 0, :], cst[:, 1, :], cst[:, 2, :], cst[:, 3, :]
        cols = sb(es, "cols", [128, NCOLS], F32)
        colsb = k.buf("cols")
        dcol = sb(es, "dcol", [128, NDC], F32)
        dcolb = k.buf("dcol")
        bias31 = sb(es, "bias31", [128, 8], F32)
        bias31b = k.buf("bias31")
        gmix = sb(es, "gmix", [128, D], F32)
        gmixb = k.buf("gmix")
        gmlp = sb(es, "gmlp", [128, D], F32)
        gmlpb = k.buf("gmlp")
        wrg = sb(es, "wrg", [128, 2, 8, 128], BF16)
        wrgb = k.buf("wrg")
        gsubT = sb(es, "gsubT", [128, 128], F32)
        gsubTb = k.buf("gsubT")

        k.dma("sp", cols[:], cols_d[:], colsb, writes=[colsb])
        k.dma("pool", cst[:].rearrange("p a b -> p (a b)"), cst_d[:], cstb, writes=[cstb])
        k.dma("sp", bias31[:], bcast_rows(relb, 128, 31 * 8, 8), bias31b, writes=[bias31b])
        k.dma("sp", gmix[:], bcast_rows(gmix_d, 128, 0, D), gmixb, writes=[gmixb])
        k.dma("sp", gmlp[:], bcast_rows(gmlp_d, 128, 0, D), gmlpb, writes=[gmlpb])
        k.dma("pool", wrg[:].rearrange("p a c f -> p (a c f)"), w_rg[:], wrgb, writes=[wrgb])
        k.dma("sp", gsubT[:], bcast_rows(subg_d, 128, 0, 128), gsubTb, writes=[gsubTb])
        k.op("dve", lambda e: e.tensor_scalar(out=gsubT[:], in0=gsubT[:], scalar1=0.8, scalar2=None, op0=ALU.mult), reads=[gsubTb], writes=[gsubTb])

        vsb = k.buf("vscr")
        with ExitStack() as s0:
            lamv = sb(s0, "lamv", [128, 256], F32)
            lamvb = k.buf("lamv")
            rb33 = sb(s0, "rb33", [33, 8], F32)
            rb33b = k.buf("rb33")
            oht = sb(s0, "oht", [33, 383], F32)
            ohtb = k.buf("oht")
            vecs = sb(s0, "vecs", [8, 383], F32)
            vecsb = k.buf("vecs")
            tmpc = sb(s0, "tmpc", [128, 64], F32)
            tmpcb = k.buf("tmpc")
            junk = sb(s0, "junk", [128, 64], F32)
            junkb = k.buf("junk")
            vps = ps(s0, "vps", [8, 383], F32)
            vpsb = k.buf("vps")
            k.dma("sp", lamv[:], bcast_rows(lamv_d, 128, 0, 256), lamvb, writes=[lamvb])
            k.op("dve", lambda e: e.memset(rb33[:], NEG), writes=[rb33b])
            k.dma("sp", rb33[0:32, :], relb[:], rb33b, writes=[rb33b])
            k.dma("sp", oht[:], oht_d[:], ohtb, writes=[ohtb])
            k.op("dve", lambda e: e.tensor_tensor(out=dcol[:, D_GK:D_GK + 1], in0=cols[:, C_GQ:C_GQ + 1],
                                                  in1=cols[:, C_GK:C_GK + 1], op=ALU.mult), reads=[colsb], writes=[dcolb])
            k.op("dve", lambda e: e.tensor_scalar(out=dcol[:, D_GSUB:D_GSUB + 1], in0=cols[:, C_SUB:C_SUB + 1],
                                                  scalar1=0.8, scalar2=None, op0=ALU.mult), reads=[colsb], writes=[dcolb], partial=True)
            k.op("dve", lambda e: e.tensor_tensor(out=junk[:, 0:64], in0=lamv[:, 0:64], in1=lamv[:, 64:128], op=ALU.mult),
                 reads=[lamvb], writes=[junkb])
            k.op("dve", lambda e: e.reduce_sum(out=tmpc[:, 0:1], in_=junk[:, 0:64], axis=mybir.AxisListType.X),
                 reads=[junkb], writes=[tmpcb])
            k.op("dve", lambda e: e.tensor_tensor(out=junk[:, 0:64], in0=lamv[:, 128:192], in1=lamv[:, 192:256], op=ALU.mult),
                 reads=[lamvb], writes=[junkb])
            k.op("dve", lambda e: e.reduce_sum(out=tmpc[:, 1:2], in_=junk[:, 0:64], axis=mybir.AxisListType.X),
                 reads=[junkb], writes=[tmpcb], partial=True)
            k.op("act", lambda e: e.activation(out=tmpc[:, 2:4], in_=tmpc[:, 0:2], func=AF.Exp), reads=[tmpcb], writes=[tmpcb])
            k.op("dve", lambda e: e.tensor_tensor(out=tmpc[:, 4:5], in0=tmpc[:, 3:4], in1=tmpc[:, 2:3], op=ALU.subtract),
                 reads=[tmpcb], writes=[tmpcb])
            k.op("dve", lambda e: e.tensor_scalar(out=dcol[:, D_NLAM:D_NLAM + 1], in0=tmpc[:, 4:5], scalar1=-0.2, scalar2=None,
                                                  op0=ALU.add), reads=[tmpcb], writes=[dcolb], partial=True)
            k.op("dve", lambda e: e.tensor_scalar(out=dcol[:, D_HBA:D_HBA + 16], in0=cols[:, C_BA:C_BA + 16], scalar1=0.5,
                                                  scalar2=None, op0=ALU.mult), reads=[colsb], writes=[dcolb], partial=True)
            yv, zv, z2, acc = tmpc[:, 8:16], tmpc[:, 16:24], tmpc[:, 24:32], tmpc[:, 32:40]
            k.op("act", lambda e: e.activation(out=yv, in_=cols[:, C_LAM:C_LAM + 8], func=AF.Exp, scale=-1.0),
                 reads=[colsb], writes=[tmpcb])
            k.op("dve", lambda e: e.tensor_scalar(out=zv, in0=yv, scalar1=2.0, scalar2=None, op0=ALU.add), reads=[tmpcb], writes=[tmpcb])
            k.op("dve", lambda e: e.reciprocal(out=zv, in_=zv), reads=[tmpcb], writes=[tmpcb])
            k.op("dve", lambda e: e.tensor_tensor(out=zv, in0=zv, in1=yv, op=ALU.mult), reads=[tmpcb], writes=[tmpcb])
            k.op("dve", lambda e: e.tensor_tensor(out=z2, in0=zv, in1=zv, op=ALU.mult), reads=[tmpcb], writes=[tmpcb])
            k.op("dve", lambda e: e.memset(acc, 1.0 / 19.0), reads=[tmpcb], writes=[tmpcb])
            for n in (17, 15, 13, 11, 9, 7, 5, 3, 1):
                k.op("dve", lambda e: e.tensor_tensor(out=acc, in0=acc, in1=z2, op=ALU.mult), reads=[tmpcb], writes=[tmpcb])
                k.op("dve", lambda e, n=n: e.tensor_scalar(out=acc, in0=acc, scalar1=1.0 / n, scalar2=None, op0=ALU.add),
                     reads=[tmpcb], writes=[tmpcb])
            k.op("dve", lambda e: e.tensor_tensor(out=acc, in0=acc, in1=zv, op=ALU.mult), reads=[tmpcb], writes=[tmpcb])
            k.op("dve", lambda e: e.tensor_scalar(out=dcol[:, D_M4:D_M4 + 8], in0=acc, scalar1=-8.0, scalar2=None, op0=ALU.mult),
                 reads=[tmpcb], writes=[dcolb], partial=True)
            k.op("dve", lambda e: e.tensor_scalar(out=dcol[:, D_M8:D_M8 + 8], in0=acc, scalar1=-16.0, scalar2=None, op0=ALU.mult),
                 reads=[tmpcb], writes=[dcolb], partial=True)
            k.op("pe", lambda e: e.matmul(vps[:], lhsT=rb33[:], rhs=oht[:], start=True, stop=True), reads=[rb33b, ohtb], writes=[vpsb])
            k.op("dve", lambda e: e.tensor_copy(out=vecs[:], in_=vps[:]), reads=[vpsb], writes=[vecsb])
            k.dma("sp", vscr[:], vecs[:], vsb, reads=[vecsb], writes=[vsb])
            k.op("dve", lambda e: e.memset(junk[:, 0:1], 0.0), reads=[vsb, tmpcb, lamvb, rb33b, ohtb], writes=[junkb])
        setup_fence = dict(k.cnt)

        winrb, wpab, wprb, wupb, woutb, wdownb = (k.buf(n) for n in ("winr", "wpa", "wpr", "wup", "wout", "wdown"))
        conv_jobs = []
        for cc in range(32):
            conv_jobs.append((winr_t[cc], w_in[:, 3072 + 128 * cc:3072 + 128 * (cc + 1)].rearrange("(c p) n -> p c n", p=128), winrb))
        for j in range(8):
            conv_jobs.append((wpa_t[j], w_pa[:, 128 * j:128 * (j + 1)].rearrange("(c p) n -> p c n", p=128), wpab))
            conv_jobs.append((wpr_t[j], w_pr[:, 128 * j:128 * (j + 1)].rearrange("(c p) n -> p c n", p=128), wprb))
        conv_jobs.append((wout_t[:], w_out.rearrange("(c p) n -> p c n", p=128), woutb))
        for f in range(32):
            conv_jobs.append((wup_t[f], w_up[:, 128 * f:128 * (f + 1)].rearrange("(c p) n -> p c n", p=128), wupb))
        for q4 in range(4):
            conv_jobs.append((wdown_t[:, 8 * q4:8 * (q4 + 1), :],
                              w_down[1024 * q4:1024 * (q4 + 1), :].rearrange("(c p) n -> p c n", p=128), wdownb))

        def issue_conv(n):
            for _ in range(n):
                if conv_jobs:
                    o, i, b = conv_jobs.pop(0)
                    k.dma("pool", o, i, b, writes=[b], partial=True)

        def norm_a(st_x, xbuf, gt, gtb, xn, xnb, st, stb, junkt, junktb):
            jb = junktb if isinstance(junktb, list) else [junktb]
            k.op("act", lambda e: e.activation(out=junkt, in_=st_x, func=AF.Square, accum_out=st[:, 0:1]),
                 reads=[xbuf], writes=jb + [stb])
            k.op("act", lambda e: e.activation(out=st[:, 1:2], in_=st[:, 0:1], func=AF.Ln, bias=EPS, scale=1.0 / D),
                 reads=[stb], writes=[stb])
            k.op("act", lambda e: e.activation(out=st[:, 2:3], in_=st[:, 1:2], func=AF.Exp, scale=-0.5), reads=[stb], writes=[stb])
            k.op("dve", lambda e: e.scalar_tensor_tensor(out=xn[:], in0=st_x, scalar=st[:, 2:3], in1=gt[:], op0=ALU.mult, op1=ALU.mult),
                 reads=[xbuf, stb, gtb], writes=[xnb])

        def norm_b(dst_fn, dstb, tp_ps, tpb, xn, xnb, evac_eng):
            k.op("pe", [lambda e, c=c: e.transpose(out=tp_ps[:, c, :], in_=xn[:, 128 * c:128 * (c + 1)], identity=ident) for c in range(8)],
                 reads=[xnb, cstb], writes=[tpb])
            k.op(evac_eng, lambda e: e.tensor_copy(out=dst_fn(), in_=tp_ps[:]) if evac_eng != "act" else e.copy(out=dst_fn(), in_=tp_ps[:]),
                 reads=[tpb], writes=[dstb], partial=True)

        def norm_transpose(st_x, xbuf, gt, gtb, dst_fn, dstb, tp_ps, tpb, xn, xnb, st, stb, junkt, junktb, evac_eng):
            norm_a(st_x, xbuf, gt, gtb, xn, xnb, st, stb, junkt[:] if hasattr(junkt, "shape") and len(junkt.shape) == 2 else junkt, junktb)
            norm_b(dst_fn, dstb, tp_ps, tpb, xn, xnb, evac_eng)

        with ExitStack() as s12:
            hT = sb(s12, "hT", [128, 8, S], BF16)
            hTb = k.bufs(NCH, "hT")
            with ExitStack() as s1:
                xt = [sb(s1, f"xt{i}", [128, D], F32) for i in range(3)]
                xtb = k.bufs(3, "xt")
                xn = [sb(s1, f"xn{i}", [128, D], BF16) for i in range(2)]
                xnb = k.bufs(2, "xn")
                st = [sb(s1, f"st{i}", [128, 4], F32) for i in range(2)]
                stb = k.bufs(2, "st")
                junkt = sb(s1, "junkt", [128, D], BF16)
                junktb = k.buf("junkt")
                tp = [ps(s1, f"tp{i}", [128, 8, 128], BF16) for i in range(2)]
                tpb = k.bufs(2, "tp")
                k.fence(xtb + xnb + stb + [junktb] + tpb)
                issue_conv(0)
                for i in range(NT):
                    s3i, s2i = i % 3, i % 2
                    k.dma("sp", xt[s3i][:], x[128 * i:128 * (i + 1), :], xtb[s3i], writes=[xtb[s3i]])
                    norm_transpose(xt[s3i][:], xtb[s3i], gmix, gmixb,
                                   lambda i=i: hT[:, :, 128 * i:128 * (i + 1)], hTb[i // 4],
                                   tp[s2i], tpb[s2i], xn[s2i], xnb[s2i], st[s2i], stb[s2i], junkt, junktb,
                                   "dve" if i % 2 == 0 else "act")
                k.op("dve", lambda e: e.memset(st[0][:, 3:4], 0.0), reads=xtb + xnb + [junktb] + tpb + stb, writes=[stb[0]])

            with ExitStack() as s2:
                qT = sb(s2, "qT", [128, S], BF16)
                qTb = k.bufs(NCH, "qT")
                kT = sb(s2, "kT", [128, S], BF16)
                kTb = k.bufs(NCH, "kT")
                vv = sb(s2, "vv", [128, NT, 129], BF16)
                vvb = k.bufs(NCH, "vv")
                wqkv = [sb(s2, f"wqkv{i}", [128, 3, 8, 128], BF16) for i in range(2)]
                wqkvb = k.bufs(2, "wqkv")
                Pt = [sb(s2, f"P{i}", [128, 2, 512], BF16) for i in range(3)]
                Ptb = k.bufs(3, "P")
                tmpdb = []
                sq = [sb(s2, f"sq{i}", [128, 512], BF16) for i in range(2)]
                sqb = k.bufs(2, "sq")
                rs = [sb(s2, f"rs{i}", [128, 512], F32) for i in range(2)]
                rsb = k.bufs(2, "rs")
                zz = sb(s2, "zz", [128, 16], F32)
                zzb = k.buf("zz")
                tmpo = sb(s2, "tmpo", [128, 128], F32)
                tmpob = k.buf("tmpo")
                dd = sb(s2, "dd", [128, 4, 128], F32)
                ddb = k.buf("dd")
                abf = sb(s2, "abf", [128, 4, 128], BF16)
                abfb = k.buf("abf")
                junk2 = sb(s2, "junk2", [128, 4, 128], F32)
                junk2b = k.buf("junk2")
                hank = [sb(s2, f"hank{i}", [128, 2, 128], F32) for i in range(2)]
                hankb = k.bufs(2, "hank")
                T01 = [sb(s2, f"T01{i}", [128, 2, 128], F32) for i in range(2)]
                T01b = k.bufs(2, "T01")
                E01 = [sb(s2, f"E01{i}", [128, 2, 128], F32) for i in range(2)]
                E01b = k.bufs(2, "E01")
                nb31 = sb(s2, "nb31", [128, 8], F32)
                nb31b = k.buf("nb31")
                ozc = sb(s2, "ozc", [128, 3, 387], F32)
                ozcb = k.buf("ozc")
                PS = [ps(s2, f"PS{i}", [128, 2, 512], F32) for i in range(2)]
                OZ = ps(s2, "OZ", [128, 3, 512], F32)
                TPp = ps(s2, "TPp", [128, 4, 128], BF16)
                PS.append(OZ)
                PSb = k.bufs(3, "PS")
                OZb = PSb[2]
                TPb = k.buf("TPp")
                k.fence(qTb + kTb + vvb + wqkvb + Ptb + tmpdb + sqb + rsb + [zzb, ddb, tmpob, abfb, junk2b, TPb] + hankb + T01b + E01b + [nb31b, ozcb] + PSb)
                k.op("dve", lambda e: e.memset(vv[:, :, 128:129], 1.0), writes=vvb)
                k.op("dve", lambda e: e.tensor_scalar(out=nb31[:], in0=bias31[:], scalar1=-1.0, scalar2=None, op0=ALU.mult), reads=[bias31b], writes=[nb31b])

                def acc(i, c):
                    a_ = 2 * i + c
                    return OZ[:, a_ // 3, 129 * (a_ % 3):129 * (a_ % 3) + 129]

                def load_head_w(h):
                    sl = h % 2
                    for wi in range(3):
                        c0 = 1024 * wi + 128 * h
                        k.dma("pool", wqkv[sl][:, wi, :, :], w_in[:, c0:c0 + 128].rearrange("(c p) n -> p c n", p=128),
                              wqkvb[sl], writes=[wqkvb[sl]], partial=(wi > 0))
                    k.dma("sp", hank[sl][:, 0, :], bass.AP(vscr.tensor, 383 * h + 128, [[1, 128], [1, 128]]), hankb[sl],
                          reads=[vsb], writes=[hankb[sl]])
                    k.dma("sp", hank[sl][:, 1, :], bass.AP(vscr.tensor, 383 * h, [[1, 128], [1, 128]]), hankb[sl],
                          reads=[vsb], writes=[hankb[sl]], partial=True)

                load_head_w(0)
                gcnt = [0, 0, 0, 0]
                scnt = [0]
                cur_slot = [0]
                for h in range(H):
                    sl = h % 2
                    if h + 1 < H:
                        load_head_w(h + 1)
                    issue_conv(11)
                    k.op("dve", lambda e: e.tensor_copy(out=T01[sl][:], in_=bass.AP(hank[sl], 127, [[256, 128], [128, 2], [-1, 128]])),
                         reads=[hankb[sl]], writes=[T01b[sl]])
                    k.op("act", lambda e: e.activation(out=E01[sl][:], in_=T01[sl][:], func=AF.Exp, bias=nb31[:, h:h + 1], scale=1.0),
                         reads=[T01b[sl], nb31b], writes=[E01b[sl]])
                    def proj_a(wi, t):
                        pi = gcnt[0] % 3
                        gcnt[0] += 1
                        si = gcnt[1] % 2
                        gcnt[1] += 1
                        P_, Pb_ = PS[pi], PSb[pi]
                        k.op("pe", [lambda e, c=c: e.matmul(P_[:, 0, :], lhsT=wqkv[sl][:, wi, c, :], rhs=hT[:, c, 512 * t:512 * (t + 1)],
                                                            start=(c == 0), stop=(c == 7)) for c in range(8)],
                             reads=[wqkvb[sl], hTb[t]], writes=[Pb_])
                        k.op("act", lambda e: e.activation(out=sq[si][:], in_=P_[:, 0, :], func=AF.Square), reads=[Pb_], writes=[sqb[si]])
                        return (wi, t, pi, si)

                    def proj_b(st_):
                        wi, t, pi, si = st_
                        P_, Pb_ = PS[pi], PSb[pi]
                        k.op("pe", lambda e: e.matmul(P_[:, 1, :], lhsT=bd64, rhs=sq[si][:], start=True, stop=True),
                             reads=[sqb[si], cstb], writes=[Pb_], partial=True)
                        k.op("act", lambda e: e.activation(out=rs[si][:], in_=P_[:, 1, :], func=AF.Ln, bias=EPS), reads=[Pb_], writes=[rsb[si]])
                        k.op("act", lambda e: e.activation(out=rs[si][:], in_=rs[si][:], func=AF.Exp, scale=-0.5), reads=[rsb[si]], writes=[rsb[si]])
                        if wi == 0:
                            k.op("dve", lambda e: e.tensor_tensor(out=qT[:, 512 * t:512 * (t + 1)], in0=P_[:, 0, :], in1=rs[si][:], op=ALU.mult),
                                 reads=[Pb_, rsb[si]], writes=[qTb[t]])
                        else:
                            k.op("dve", lambda e: e.scalar_tensor_tensor(out=kT[:, 512 * t:512 * (t + 1)], in0=P_[:, 0, :],
                                                                         scalar=dcol[:, D_GK:D_GK + 1], in1=rs[si][:], op0=ALU.mult, op1=ALU.mult),
                                 reads=[Pb_, rsb[si], dcolb], writes=[kTb[t]])

                    plist = [(wi, t) for wi in range(2) for t in range(NCH)]
                    prev = None
                    for (wi, t) in plist:
                        cur = proj_a(wi, t)
                        if prev is not None:
                            proj_b(prev)
                        prev = cur
                    proj_b(prev)
                    for tg in range(NCH):
                        pi = gcnt[0] % 3
                        gcnt[0] += 1
                        P_, Pb_ = PS[pi], PSb[pi]
                        fl = []
                        for i4 in range(4):
                            tt = 4 * tg + i4
                            for c in range(8):
                                fl.append(lambda e, c=c, i4=i4, tt=tt: e.matmul(P_[:, 0, 128 * i4:128 * (i4 + 1)], lhsT=hT[:, c, 128 * tt:128 * (tt + 1)],
                                                                               rhs=wqkv[sl][:, 2, c, :], start=(c == 0), stop=(c == 7)))
                        k.op("pe", fl, reads=[wqkvb[sl], hTb[tg]], writes=[Pb_])
                        k.op("dve", lambda e: e.tensor_copy(out=vv[:, 4 * tg:4 * tg + 4, 0:128], in_=P_[:, 0, :].rearrange("p (a b) -> p a b", b=128)),
                             reads=[Pb_], writes=[vvb[tg]], partial=True)
                    iters = []
                    for t in range(NCH):
                        full = list(range(0, max(0, 4 * t - 1)))
                        diag = list(range(max(0, 4 * t - 1), 4 * t + 4))
                        order = []
                        if full:
                            order.append(full.pop(0))
                        while full or diag:
                            if diag:
                                order.append(diag.pop(0))
                            if full:
                                order.append(full.pop(0))
                        iters += [(t, j) for j in order]
                    lastj = {}
                    for (t, j) in iters:
                        for i in range(max(0, j - 4 * t), 4):
                            lastj[(t, i)] = j
                    sidx = {}

                    pending = []

                    def emit_qk(n):
                        t, j = iters[n]
                        si = scnt[0] % 2
                        scnt[0] += 1
                        sidx[n] = si
                        c0 = max(0, 128 * (j - 4 * t))
                        k.op("pe", [lambda e, c=c: e.matmul(PS[si][:, c, c0:512], lhsT=kT[64 * c:64 * (c + 1), 128 * j:128 * (j + 1)],
                                                            rhs=qT[64 * c:64 * (c + 1), 512 * t + c0:512 * (t + 1)], start=True, stop=True)
                                    for c in range(2)],
                             reads=[kTb[j // 4], qTb[t]], writes=[PSb[si]])

                    emit_qk(0)
                    emit_qk(1)
                    for n, (t, j) in enumerate(iters):
                        si = sidx[n]
                        cur_slot[0] = si
                        Sp, Spb = PS[si], PSb[si]
                        pi = gcnt[2] % 3
                        gcnt[2] += 1
                        Pp, Ppb = Pt[pi], Ptb[pi]
                        jj = j - 4 * t
                        bcol = bias31[:, h:h + 1]
                        c0 = max(0, 128 * jj)
                        k.op("act", lambda e: e.activation(out=Pp[:, :, c0:512], in_=Sp[:, :, c0:512], func=AF.Exp, bias=bcol, scale=0.125),
                             reads=[Spb, bias31b], writes=[Ppb])
                        if jj >= -1:
                            if jj == -1:
                                cb, toff, nb = 0, 128, 1
                            elif jj == 3:
                                cb, toff, nb = 384, 0, 1
                            else:
                                cb, toff, nb = 128 * jj, 0, 2
                            k.op("dve", lambda e: e.tensor_tensor(out=Pp[:, :, cb:cb + 128 * nb], in0=Pp[:, :, cb:cb + 128 * nb],
                                                                  in1=bass.AP(E01[sl], toff, [[256, 128], [0, 2], [1, 128 * nb]]), op=ALU.mult),
                                 reads=[Ppb, E01b[sl]], writes=[Ppb])
                        if n + 2 < len(iters):
                            emit_qk(n + 2)
                        first = (n == 0 or iters[n - 1][0] != t)
                        fl = []
                        for i in range(max(0, jj), 4):
                            for c in range(2):
                                fl.append(lambda e, i=i, c=c: e.matmul(acc(i, c), lhsT=Pp[:, c, 128 * i:128 * (i + 1)], rhs=vv[:, j, :],
                                                                       start=(first and (2 * i + c) % 3 == 0), stop=(j == lastj[(t, i)]),
                                                                       skip_group_check=True))
                        for _ in range(NDUMMY):
                            fl.append(lambda e: e.matmul(OZ[:, 2, 258:512], lhsT=ident, rhs=kT[:, 0:254], start=False, stop=False, skip_group_check=True))
                        k.op("pe", fl, reads=[Ppb, vvb[j // 4], cstb], writes=[OZb], partial=not first)
                        last = (n == len(iters) - 1 or iters[n + 1][0] != t)
                        if last:
                            k.op("dve", lambda e: e.tensor_copy(out=ozc[:], in_=OZ[:, :, 0:387]), reads=[OZb], writes=[ozcb])
                            zcols = bass.AP(ozc, 128, [[1161, 128], [387, 3], [129, 3]])

                            def accs(i, c):
                                a_ = 2 * i + c
                                return ozc[:, a_ // 3, 129 * (a_ % 3):129 * (a_ % 3) + 128]

                            k.op("dve", lambda e: e.reciprocal(out=zz[:, 0:9].rearrange("p (a b) -> p a b", b=3), in_=zcols), reads=[ozcb], writes=[zzb])
                            k.op("dve", lambda e: e.tensor_scalar(out=zz[:, 0:8].rearrange("p (i c) -> p i c", c=2)[:, :, 1:2],
                                                                  in0=zz[:, 0:8].rearrange("p (i c) -> p i c", c=2)[:, :, 1:2],
                                                                  scalar1=dcol[:, D_NLAM:D_NLAM + 1], scalar2=None, op0=ALU.mult),
                                 reads=[zzb, dcolb], writes=[zzb])
                            for i in range(4):
                                k.op("dve", lambda e, i=i: e.tensor_scalar(out=tmpo[:], in0=accs(i, 0), scalar1=zz[:, 2 * i:2 * i + 1], scalar2=None,
                                                                           op0=ALU.mult), reads=[ozcb, zzb], writes=[tmpob])
                                k.op("dve", lambda e, i=i: e.scalar_tensor_tensor(out=dd[:, i, :], in0=accs(i, 1), scalar=zz[:, 2 * i + 1:2 * i + 2],
                                                                                  in1=tmpo[:], op0=ALU.mult, op1=ALU.add),
                                     reads=[ozcb, zzb, tmpob], writes=[ddb], partial=(i > 0))

                            def part2a(t=t):
                                k.op("dve", lambda e: e.tensor_tensor(out=junk2[:], in0=dd[:], in1=dd[:], op=ALU.mult), reads=[ddb], writes=[junk2b])
                                k.op("dve", lambda e: e.reduce_sum(out=zz[:, 8:12], in_=junk2[:], axis=mybir.AxisListType.X), reads=[junk2b], writes=[zzb], partial=True)
                                k.op("act", lambda e: e.activation(out=zz[:, 12:16], in_=zz[:, 8:12], func=AF.Ln, bias=EPS, scale=1.0 / 128), reads=[zzb], writes=[zzb])
                                k.op("act", lambda e: e.activation(out=zz[:, 12:16], in_=zz[:, 12:16], func=AF.Exp, scale=-0.5), reads=[zzb], writes=[zzb])
                                for i in range(4):
                                    k.op("dve", lambda e, i=i: e.scalar_tensor_tensor(out=abf[:, i, :], in0=dd[:, i, :], scalar=zz[:, 12 + i:13 + i], in1=gsubT[:],
                                                                                      op0=ALU.mult, op1=ALU.mult),
                                         reads=[ddb, zzb, gsubTb], writes=[abfb], partial=(i > 0))

                            def part2b(t=t):
                                k.op("pe", [lambda e, i=i: e.transpose(out=TPp[:, i, :], in_=abf[:, i, :], identity=ident) for i in range(4)],
                                     reads=[abfb, cstb], writes=[TPb])
                                k.op("dve", lambda e: e.tensor_copy(out=AT[:, h, 512 * t:512 * (t + 1)], in_=TPp[:].rearrange("p a b -> p (a b)")),
                                     reads=[TPb], writes=[ATb[h][t]])

                            pending.append((n + 3, part2a))
                            pending.append((n + 7, part2b))
                        while pending and (pending[0][0] <= n or n == len(iters) - 1):
                            pending.pop(0)[1]()
                issue_conv(1000)
                k.op("dve", lambda e: e.memset(dd[:, 0:1], 0.0),
                     reads=qTb + kTb + vvb + wqkvb + Ptb + tmpdb + sqb + rsb + [zzb, tmpob, abfb, junk2b, TPb, ozcb, nb31b] + hankb + T01b + E01b + PSb + hTb, writes=[ddb])

        with ExitStack() as s3:
            xres = [sb(s3, f"xres{i}", [128, D], F32) for i in range(4)]
            xresb = k.bufs(4, "xres")
            mgT = sb(s3, "mgT", [128, 8, 512], BF16)
            mgTb = k.bufs(8, "mgT")
            RT = sb(s3, "RT", [128, 8, 512], BF16)
            RTb = k.bufs(8, "RT")
            xn = sb(s3, "xn3", [128, D], BF16)
            xnb = k.buf("xn3")
            junkt = mgT[:, 0:2, :].rearrange("p a b -> p (a b)")
            junktb = [mgTb[0], mgTb[1]]
            st = sb(s3, "st3", [128, 4], F32)
            stb = k.buf("st3")
            carry = sb(s3, "carry", [128, 8, 4], F32)
            carryb = k.bufs(8, "carry")
            wA = [sb(s3, f"wA{i}", [128, 8, 128], BF16) for i in range(7)]
            wAb = k.bufs(7, "wA")
            wO = sb(s3, "wO", [128, 8, 512], BF16)
            wOb = k.buf("wO")
            wD = [sb(s3, f"wD{i}", [128, 4, 512], BF16) for i in range(2)]
            wDb = k.bufs(2, "wD")
            tp = ps(s3, "tp3", [128, 8, 128], BF16)
            tpb = k.buf("tp3")
            G = [ps(s3, f"G{i}", [128, 512], F32) for i in range(7)]
            Gb = k.bufs(7, "G")
            k.fence(xresb + mgTb + RTb + [xnb, stb] + carryb + wAb + [wOb] + wDb + [tpb] + Gb)
            k.op("dve", lambda e: e.memset(carry[:], 0.0), writes=carryb)
            wa_i = [0]

            def load_wA(src, srcb):
                i = wa_i[0] % 7
                wa_i[0] += 1
                k.dma("sp", wA[i][:], src, wAb[i], reads=[srcb], writes=[wAb[i]])
                return wA[i], wAb[i]

            hTc = sb(s3, "hTc", [128, 8, 512], BF16)
            hTcb = k.bufs(1, "hTc")
            xs = [sb(s3, f"xs{i}", [128, D], F32) for i in range(1)]
            xsb = k.bufs(1, "xs")
            k.fence(hTcb + xsb)
            for t in range(NCH):
                tok0 = 512 * t
                with ExitStack() as sa:
                    tnames = ["xr", "xc", "xcb16", "xg", "tr", "ti", "tg", "aa", "a2", "hs"]
                    TT = [{}, {}]
                    BB = [{}, {}]
                    for nm in tnames:
                        single = nm in ("xr", "tr", "hs")
                        for par in range(2):
                            if single and par == 1:
                                TT[1][nm], BB[1][nm] = TT[0][nm], BB[0][nm]
                                continue
                            TT[par][nm] = sb(sa, nm, [128, 515 if nm == "xr" else 512], BF16 if nm == "xcb16" else F32)
                            BB[par][nm] = k.buf(nm)
                    ta = sb(sa, "ta", [128, 512], F32)
                    m2 = sb(sa, "m2", [128, 512], F32)
                    tnP = sb(sa, "tnP", [128, 8, 512], BF16)
                    m1P = sb(sa, "m1P", [128, 8, 512], BF16)
                    B = {n: k.buf(n) for n in ("ta", "m2")}
                    tnPb = k.bufs(8, "tnP")
                    m1Pb = k.bufs(8, "m1P")
                    allb = list(B.values()) + list(BB[0].values()) + list(BB[1].values()) + tnPb + m1Pb
                    k.fence(allb)
                    def load_xres():
                        for i in range(4):
                            k.dma("sp", xres[i][:], x[tok0 + 128 * i:tok0 + 128 * (i + 1), :], xresb[i], writes=[xresb[i]])

                    if t == 0:
                        load_xres()
                    if t == 0:
                        for i in range(4):
                            norm_transpose(xres[i][:], xresb[i], gmix, gmixb, lambda i=i: hTc[:, :, 128 * i:128 * (i + 1)], hTcb[0],
                                           tp, tpb, xn, xnb, st, stb, junkt, junktb, "dve")

                    def rnn_s1a(c):
                        T_, B_ = TT[c % 2], BB[c % 2]
                        xr, xc, xcb16, xg, tg_ = T_["xr"], T_["xc"], T_["xcb16"], T_["xg"], T_["tg"]
                        wx, wxb = load_wA(winr_t[c], winrb)
                        wg, wgb = load_wA(winr_t[8 + c], winrb)
                        k.op("pe", [lambda e, d=d: e.matmul(G[0][:], lhsT=wx[:, d, :], rhs=hTc[:, d, :], start=(d == 0), stop=(d == 7)) for d in range(8)],
                             reads=[wxb, hTcb[0]], writes=[Gb[0]])
                        k.op("pe", [lambda e, d=d: e.matmul(G[1][:], lhsT=wg[:, d, :], rhs=hTc[:, d, :], start=(d == 0), stop=(d == 7)) for d in range(8)],
                             reads=[wgb, hTcb[0]], writes=[Gb[1]])
                        k.op("act", lambda e: e.copy(out=xr[:, 3:515], in_=G[0][:]), reads=[Gb[0]], writes=[B_["xr"]])
                        k.op("act", lambda e: e.copy(out=xg[:], in_=G[1][:]), reads=[Gb[1]], writes=[B_["xg"]])
                        k.op("pool", lambda e: e.tensor_tensor(out=tg_[:], in0=xg[:], in1=xg[:], op=ALU.mult), reads=[B_["xg"]], writes=[B_["tg"]])
                        k.op("pool", lambda e: e.tensor_scalar(out=tg_[:], in0=tg_[:], scalar1=0.044715, scalar2=1.0, op0=ALU.mult, op1=ALU.add),
                             reads=[B_["tg"]], writes=[B_["tg"]])
                        k.op("pool", lambda e: e.tensor_tensor(out=tg_[:], in0=tg_[:], in1=xg[:], op=ALU.mult), reads=[B_["tg"], B_["xg"]], writes=[B_["tg"]])
                        k.op("dve", lambda e: e.tensor_copy(out=xr[:, 0:3], in_=carry[:, c, 0:3]), reads=[carryb[c]], writes=[B_["xr"]], partial=True)
                        cw = lambda tap: cols[:, C_CW + 8 * tap + c:C_CW + 8 * tap + c + 1]
                        k.op("dve", lambda e: e.tensor_scalar(out=xc[:], in0=xr[:, 0:512], scalar1=cw(0), scalar2=cols[:, C_CB + c:C_CB + c + 1],
                                                              op0=ALU.mult, op1=ALU.add), reads=[B_["xr"], colsb], writes=[B_["xc"]])
                        for tap in (1, 2, 3):
                            k.op("dve", lambda e, tap=tap: e.scalar_tensor_tensor(out=xc[:], in0=xr[:, tap:tap + 512], scalar=cw(tap), in1=xc[:],
                                                                                  op0=ALU.mult, op1=ALU.add), reads=[B_["xr"], B_["xc"], colsb], writes=[B_["xc"]])
                        k.op("dve", lambda e: e.tensor_copy(out=carry[:, c, 0:3], in_=xr[:, 512:515]), reads=[B_["xr"]], writes=[carryb[c]])
                        k.op("pool", lambda e: e.tensor_copy(out=xcb16[:], in_=xc[:]), reads=[B_["xc"]], writes=[B_["xcb16"]])

                    def rnn_s2(c):
                        T_, B_ = TT[c % 2], BB[c % 2]
                        xcb16, xg, tr, ti_, tg_, aa, a2 = T_["xcb16"], T_["xg"], T_["tr"], T_["ti"], T_["tg"], T_["aa"], T_["a2"]
                        k.op("pe", lambda e: e.matmul(G[2][:], lhsT=wrg[:, 0, c, :], rhs=xcb16[:], start=True, stop=True),
                             reads=[wrgb, B_["xcb16"]], writes=[Gb[2]])
                        k.op("pe", lambda e: e.matmul(G[3][:], lhsT=wrg[:, 1, c, :], rhs=xcb16[:], start=True, stop=True),
                             reads=[wrgb, B_["xcb16"]], writes=[Gb[3]])
                        k.op("act", lambda e: e.activation(out=tr[:], in_=G[2][:], func=AF.Tanh, bias=dcol[:, D_HBA + c:D_HBA + c + 1], scale=0.5),
                             reads=[Gb[2], dcolb], writes=[B_["tr"]])
                        k.op("act", lambda e: e.activation(out=ti_[:], in_=G[3][:], func=AF.Tanh, bias=dcol[:, D_HBX + c:D_HBX + c + 1], scale=0.5),
                             reads=[Gb[3], dcolb], writes=[B_["ti"]])
                        k.op("act", lambda e: e.activation(out=tg_[:], in_=tg_[:], func=AF.Tanh, scale=math.sqrt(2.0 / math.pi)),
                             reads=[B_["tg"]], writes=[B_["tg"]])
                        k.op("act", lambda e: e.activation(out=aa[:], in_=tr[:], func=AF.Exp, bias=dcol[:, D_M4 + c:D_M4 + c + 1],
                                                           scale=dcol[:, D_M4 + c:D_M4 + c + 1]), reads=[B_["tr"], dcolb], writes=[B_["aa"]])
                        k.op("act", lambda e: e.activation(out=a2[:], in_=tr[:], func=AF.Exp, bias=dcol[:, D_M8 + c:D_M8 + c + 1],
                                                           scale=dcol[:, D_M8 + c:D_M8 + c + 1]), reads=[B_["tr"], dcolb], writes=[B_["a2"]])
                        k.op("act", lambda e: e.activation(out=a2[:], in_=a2[:], func=AF.Sqrt, bias=1.0, scale=-1.0), reads=[B_["a2"]], writes=[B_["a2"]])

                    def rnn_s3(c):
                        T_, B_ = TT[c % 2], BB[c % 2]
                        xc, xg, ti_, tg_, aa, a2, hs = T_["xc"], T_["xg"], T_["ti"], T_["tg"], T_["aa"], T_["a2"], T_["hs"]
                        k.op("dve", lambda e: e.scalar_tensor_tensor(out=ti_[:], in0=ti_[:], scalar=1.0, in1=xc[:], op0=ALU.add, op1=ALU.mult),
                             reads=[B_["ti"], B_["xc"]], writes=[B_["ti"]])
                        k.op("dve", lambda e: e.tensor_tensor(out=ti_[:], in0=ti_[:], in1=a2[:], op=ALU.mult), reads=[B_["ti"], B_["a2"]], writes=[B_["ti"]])
                        k.op("dve", lambda e: e.tensor_tensor_scan(out=hs[:], data0=aa[:], data1=ti_[:], initial=carry[:, c, 3:4], op0=ALU.mult, op1=ALU.add),
                             reads=[B_["aa"], B_["ti"], carryb[c]], writes=[B_["hs"]])
                        k.op("dve", lambda e: e.tensor_copy(out=carry[:, c, 3:4], in_=hs[:, 511:512]), reads=[B_["hs"]], writes=[carryb[c]])
                        k.op("dve", lambda e: e.scalar_tensor_tensor(out=tg_[:], in0=tg_[:], scalar=1.0, in1=xg[:], op0=ALU.add, op1=ALU.mult),
                             reads=[B_["tg"], B_["xg"]], writes=[B_["tg"]])
                        k.op("dve", lambda e: e.scalar_tensor_tensor(out=RT[:, c, :], in0=hs[:], scalar=0.25, in1=tg_[:], op0=ALU.mult, op1=ALU.mult),
                             reads=[B_["hs"], B_["tg"]], writes=[RTb[c]])

                    def merge_early(j):
                        wga_s, wga_b = load_wA(winr_t[16 + j], winrb)
                        wgn_s, wgn_b = load_wA(winr_t[24 + j], winrb)
                        wpa_s, wpa_b = load_wA(wpa_t[j], wpab)
                        k.op("pe", [lambda e, d=d: e.matmul(G[4][:], lhsT=wga_s[:, d, :], rhs=hTc[:, d, :], start=(d == 0), stop=(d == 7)) for d in range(8)],
                             reads=[wga_b, hTcb[0]], writes=[Gb[4]])
                        k.op("pe", [lambda e, d=d: e.matmul(G[5][:], lhsT=wgn_s[:, d, :], rhs=hTc[:, d, :], start=(d == 0), stop=(d == 7)) for d in range(8)],
                             reads=[wgn_b, hTcb[0]], writes=[Gb[5]])
                        k.op("pe", [lambda e, d=d: e.matmul(G[6][:], lhsT=wpa_s[:, d, :], rhs=AT[:, d, tok0:tok0 + 512], start=(d == 0), stop=(d == 7)) for d in range(8)],
                             reads=[wpa_b] + [ATb[d][t] for d in range(8)], writes=[Gb[6]])
                        k.op("act", lambda e: e.activation(out=ta[:], in_=G[4][:], func=AF.Tanh, scale=0.5), reads=[Gb[4]], writes=[B["ta"]])
                        k.op("act", lambda e: e.activation(out=tnP[:, j, :], in_=G[5][:], func=AF.Tanh, scale=0.5), reads=[Gb[5]], writes=[tnPb[j]])
                        k.op("dve", lambda e: e.scalar_tensor_tensor(out=m1P[:, j, :], in0=ta[:], scalar=1.0, in1=G[6][:], op0=ALU.add, op1=ALU.mult),
                             reads=[B["ta"], Gb[6]], writes=[m1Pb[j]])

                    for s_ in range(10):
                        if 0 <= s_ - 2 < 8:
                            rnn_s3(s_ - 2)
                        if s_ < 8:
                            rnn_s1a(s_)
                        if 0 <= s_ - 1 < 8:
                            rnn_s2(s_ - 1)
                        if 0 <= s_ - 2 < 8:
                            merge_early(s_ - 2)
                    if t > 0:
                        load_xres()
                    for j in range(8):
                        wpr_s, wpr_b = load_wA(wpr_t[j], wprb)
                        g = j % 2
                        k.op("pe", [lambda e, d=d: e.matmul(G[g][:], lhsT=wpr_s[:, d, :], rhs=RT[:, d, :], start=(d == 0), stop=(d == 7)) for d in range(8)],
                             reads=[wpr_b] + RTb, writes=[Gb[g]])
                        k.op("dve", lambda e: e.scalar_tensor_tensor(out=m2[:], in0=tnP[:, j, :], scalar=1.0, in1=G[g][:], op0=ALU.add, op1=ALU.mult),
                             reads=[tnPb[j], Gb[g]], writes=[B["m2"]])
                        k.op("dve", lambda e: e.tensor_tensor(out=mgT[:, j, :], in0=m2[:], in1=m1P[:, j, :], op=ALU.add), reads=[B["m2"], m1Pb[j]], writes=[mgTb[j]])
                    for half in range(2):
                        k.dma("sp", wO[:], wout_t[:, :, 512 * half:512 * (half + 1)], wOb, reads=[woutb], writes=[wOb])
                        for i in range(4):
                            g = 4 + (2 * half + i) % 2
                            k.op("pe", [lambda e, d=d: e.matmul(G[g][:], lhsT=mgT[:, d, 128 * i:128 * (i + 1)], rhs=wO[:, d, :], start=(d == 0), stop=(d == 7)) for d in range(8)],
                                 reads=[wOb] + mgTb, writes=[Gb[g]])
                            k.op("dve", lambda e: e.scalar_tensor_tensor(out=xres[i][:, 512 * half:512 * (half + 1)], in0=G[g][:], scalar=0.5,
                                                                         in1=xres[i][:, 512 * half:512 * (half + 1)], op0=ALU.mult, op1=ALU.add),
                                 reads=[Gb[g], xresb[i]], writes=[xresb[i]])
                    for i in range(4):
                        norm_transpose(xres[i][:], xresb[i], gmlp, gmlpb, lambda i=i: RT[:, :, 128 * i:128 * (i + 1)], RTb[i],
                                       tp, tpb, xn, xnb, st, stb, junkt, junktb, "dve")
                    k.op("dve", lambda e: e.memset(st[:, 3:4], 0.0), reads=RTb[0:4] + allb, writes=RTb[4:8] + [stb])
                with ExitStack() as sm:
                    uT = sb(sm, "uT", [128, 32, 512], BF16)
                    uTb = k.bufs(32, "uT")
                    rl = [sb(sm, f"rl{i}", [128, 512], F32) for i in range(2)]
                    rlb = k.bufs(2, "rl")
                    k.fence(uTb + rlb)
                    for f in range(32):
                        wu, wub = load_wA(wup_t[f], wupb)
                        g = 4 + f % 3
                        k.op("pe", [lambda e, d=d: e.matmul(G[g][:], lhsT=wu[:, d, :], rhs=RT[:, d, :], start=(d == 0), stop=(d == 7)) for d in range(8)],
                             reads=[wub] + RTb, writes=[Gb[g]])
                        k.op("act", lambda e: e.activation(out=rl[f % 2][:], in_=G[g][:], func=AF.Relu), reads=[Gb[g]], writes=[rlb[f % 2]])
                        k.op("pool", lambda e: e.tensor_tensor(out=uT[:, f, :], in0=rl[f % 2][:], in1=rl[f % 2][:], op=ALU.mult),
                             reads=[rlb[f % 2]], writes=[uTb[f]])
                    def pf_a(i):
                        k.dma("sp", xs[0][:], x[tok0 + 512 + 128 * i:tok0 + 512 + 128 * (i + 1), :], xsb[0], writes=[xsb[0]])
                        norm_a(xs[0][:], xsb[0], gmix, gmixb, xn, xnb, st, stb, junkt, junktb)

                    def pf_b(i):
                        norm_b(lambda i=i: hTc[:, :, 128 * i:128 * (i + 1)], hTcb[0], tp, tpb, xn, xnb, "dve")

                    for half in range(2):
                        for f4 in range(8):
                            g_ = half * 8 + f4
                            if t + 1 < NCH and g_ % 2 == 0 and g_ <= 8:
                                if g_ >= 2:
                                    pf_b(g_ // 2 - 1)
                                if g_ <= 6:
                                    pf_a(g_ // 2)
                            wi = (half * 8 + f4) % 2
                            k.dma("sp", wD[wi][:], wdown_t[:, 4 * f4:4 * f4 + 4, 512 * half:512 * (half + 1)], wDb[wi], reads=[wdownb], writes=[wDb[wi]])
                            for fi in range(4):
                                f = 4 * f4 + fi
                                k.op("pe", [lambda e, i=i: e.matmul(G[i][:], lhsT=uT[:, f, 128 * i:128 * (i + 1)], rhs=wD[wi][:, fi, :],
                                                                    start=(f == 0), stop=(f == 31)) for i in range(4)],
                                     reads=[wDb[wi], uTb[f]], writes=Gb[0:4], partial=(f != 0))
                        for i in range(4):
                            k.op("dve", lambda e: e.tensor_tensor(out=xres[i][:, 512 * half:512 * (half + 1)], in0=G[i][:],
                                                                  in1=xres[i][:, 512 * half:512 * (half + 1)], op=ALU.add),
                                 reads=[Gb[i], xresb[i]], writes=[xresb[i]])
                    for i in range(4):
                        k.dma("sp", y[tok0 + 128 * i:tok0 + 128 * (i + 1), :], xres[i][:], xresb[i], reads=[xresb[i]])
                    k.op("dve", lambda e: e.memset(st[:, 3:4], 0.0), reads=uTb + rlb, writes=[stb])
            k._wait("sp", {kk: vv_ for b in xresb for kk, vv_ in b.r.items()})
            k.op("dve", lambda e: e.memset(st[:, 3:4], 0.0), reads=xresb + mgTb + RTb + wAb + [wOb] + wDb + [tpb] + Gb + carryb + hTcb + xsb, writes=[stb])
    import os
    if os.environ.get('KDEBUG'):
        print('counts', k.cnt, 'ninstr', nc.n_instructions if not callable(nc.n_instructions) else nc.n_instructions())
    return nc


_NC_CACHE = {}


def _host_consts():
    cst = np.zeros((128, 4, 128), np.float32)
    cst[:, 0, :] = np.eye(128, dtype=np.float32)
    bd = np.zeros((128, 128), np.float32)
    bd[:64, :64] = 1.0 / 64
    bd[64:, 64:] = 1.0 / 64
    cst[:, 1, :] = bd
    cst[:, 2, :] = 1.0 / 128
    cst[:, 3, :] = 1.0
    oht = np.zeros((33, 383), np.float32)
    for m in range(383):
        n = 255 - m
        if n < 0:
            b = 32
        elif n < 16:
            b = n
        else:
            nf = np.float32(n)
            b = 16 + int(np.float32(np.log(nf / np.float32(16)) / np.float32(math.log(128 / 16)) * np.float32(16)))
            b = min(b, 31)
        oht[b, m] = 1.0
    return cst.reshape(128, 512), oht


def kernel(**inputs):
    f = lambda name: np.ascontiguousarray(np.asarray(inputs[name], dtype=np.float32))
    x = f("x")
    col = lambda v: np.ascontiguousarray(v.reshape(-1, 128).T)
    cols = np.zeros((128, NCOLS), np.float32)
    cols[:, C_GQ] = np.tile(f("q_norm_g")[0], 2)
    cols[:, C_GK] = np.tile(f("k_norm_g")[0], 2)
    cols[:, C_SUB] = f("subln_g")[0]
    cw = f("conv_w")[0]
    for tap in range(4):
        cols[:, C_CW + 8 * tap:C_CW + 8 * tap + 8] = col(cw[tap])
    cols[:, C_CB:C_CB + 8] = col(f("conv_b")[0])
    cols[:, C_BA:C_BA + 8] = col(f("b_rg_a")[0].reshape(-1))
    cols[:, C_BX:C_BX + 8] = col(f("b_rg_x")[0].reshape(-1))
    cols[:, C_LAM:C_LAM + 8] = col(f("lru_lambda")[0])
    wrg = np.stack([f("w_rg_a")[0], f("w_rg_x")[0]], axis=0)
    wrg = np.ascontiguousarray(wrg.transpose(2, 0, 1, 3).reshape(128, 2 * 8 * 128))
    lamv = np.concatenate([f("lambda_q1")[0], f("lambda_k1")[0], f("lambda_q2")[0], f("lambda_k2")[0]]).reshape(1, 256)
    cst, oht = _host_consts()
    shared = {
        "w_in": f("w_in")[0], "w_pa": f("w_proj_attn")[0], "w_pr": f("w_proj_rnn")[0], "w_out": f("w_out")[0],
        "w_up": f("w_up")[0], "w_down": f("w_down")[0], "w_rg": wrg, "cols": cols, "cst": cst, "oht": oht,
        "rel_bias": f("rel_bias"), "lamv": np.ascontiguousarray(lamv), "gmix": f("norm_mix_g"), "gmlp": f("norm_mlp_g"), "subg": f("subln_g"),
    }
    if "nc" not in _NC_CACHE:
        _NC_CACHE["nc"] = build_nc()
    nc = _NC_CACHE["nc"]
    in_maps = [dict(shared, x=np.ascontiguousarray(x[b])) for b in range(8)]
    res = run_bass_kernel_spmd(nc, in_maps, core_ids=list(range(8)))
    return np.stack([np.asarray(r["y"], dtype=np.float32) for r in res.results], axis=0)
```
